# Optimizing a Trainium2 kernel written in Bass

```python
import math
import jax, jax.numpy as jnp
from jax import lax
import numpy as np

D_MODEL = 2048
BATCH = 1
SEQ = 16384
DEPTH = 1
DEC_BATCH = 4
DEC_SEQ = 2048
PAST_LEN = 128

ATTN_WIDTH = D_MODEL // 2
HEAD_DIM = 64
N_HEADS = ATTN_WIDTH // HEAD_DIM
POOL_WIDTH = D_MODEL - ATTN_WIDTH
POOL_WINDOWS = (2, 4, 8, 16)
N_POOL_GROUPS = len(POOL_WINDOWS)
POOL_GROUP = POOL_WIDTH // N_POOL_GROUPS
DILATED_PATTERNS = ((128, 1), (512, 4), (2048, 16))
D_FF = 4 * D_MODEL
IN_WIDTH = 3 * ATTN_WIDTH + POOL_WIDTH
EPS = 1e-6
NEG = -1e30

kernel_name = "hybrid_dilated_attn_multiscale_pool_encoder"


def _rmsnorm(x, g):
    xf = x.astype(jnp.float32)
    y = xf * lax.rsqrt(jnp.mean(xf * xf, axis=-1, keepdims=True) + EPS)
    return (y * g.astype(jnp.float32)).astype(x.dtype)


def _alibi_slopes(n_heads):
    return jnp.asarray([2.0 ** (-8.0 * (h + 1) / n_heads) for h in range(n_heads)], dtype=jnp.float32)


def _block_size(L, r):
    bs = 1
    while bs * 2 <= r and L % (bs * 2) == 0:
        bs *= 2
    return bs


def _dilated_branch(q, k, v, slopes, window, dilation):
    B, T, H, Dh = q.shape
    d = dilation
    r = window // (2 * d)
    L = T // d
    bs = _block_size(L, r)
    nb = -(-r // bs)
    nblk = L // bs
    Kw = (2 * nb + 1) * bs
    pad = nb * bs

    def split(a):
        return a.reshape(B, L, d, H, Dh).transpose(0, 2, 3, 1, 4)

    qb = split(q).astype(jnp.float32).reshape(B, d, H, nblk, bs, Dh)
    kp = jnp.pad(split(k), ((0, 0), (0, 0), (0, 0), (pad, pad), (0, 0)))
    vp = jnp.pad(split(v), ((0, 0), (0, 0), (0, 0), (pad, pad), (0, 0)))
    idx = jnp.arange(nblk)[:, None] * bs + jnp.arange(Kw)[None, :]
    kw = kp[:, :, :, idx, :].astype(jnp.float32)
    vw = vp[:, :, :, idx, :].astype(jnp.float32)

    s = jnp.einsum('bchnqe,bchnke->bchnqk', qb, kw)
    qpos = jnp.arange(nblk)[:, None] * bs + jnp.arange(bs)[None, :]
    kpos = idx - pad
    dist = jnp.abs(qpos[:, :, None] - kpos[:, None, :])
    valid = (dist <= r) & (kpos[:, None, :] >= 0) & (kpos[:, None, :] < L)
    bias = -slopes[:, None, None, None] * (dist * d).astype(jnp.float32)[None]
    s = jnp.where(valid[None, None, None], s + bias[None, None], NEG)
    m = jnp.max(s, axis=-1, keepdims=True)
    p = jnp.exp(s - m)
    den = jnp.sum(p, axis=-1, keepdims=True)
    o = jnp.einsum('bchnqk,bchnke->bchnqe', p, vw) / den
    lse = (m + jnp.log(den))[..., 0]
    o = o.reshape(B, d, H, L, Dh).transpose(0, 3, 1, 2, 4).reshape(B, T, H, Dh)
    lse = lse.reshape(B, d, H, L).transpose(0, 3, 1, 2).reshape(B, T, H)
    return o, lse


def _dilated_attention(q, k, v):
    slopes = _alibi_slopes(q.shape[2])
    outs, lses = [], []
    for window, dil in DILATED_PATTERNS:
        o, lse = _dilated_branch(q, k, v, slopes, window, dil)
        outs.append(o)
        lses.append(lse)
    w = jax.nn.softmax(jnp.stack(lses, axis=0), axis=0)
    out = jnp.sum(w[..., None] * jnp.stack(outs, axis=0), axis=0)
    return out.astype(q.dtype)


def _multiscale_pool(u, pool_w, pool_scale):
    B, T, C = u.shape
    uf = u.astype(jnp.float32)
    cs = jnp.concatenate([jnp.zeros((B, 1, C), jnp.float32), jnp.cumsum(uf, axis=1)], axis=1)
    ug = uf.reshape(B, T, N_POOL_GROUPS, POOL_GROUP)
    csg = cs.reshape(B, T + 1, N_POOL_GROUPS, POOL_GROUP)
    t = jnp.arange(T)
    outs = []
    for g, w in enumerate(POOL_WINDOWS):
        lo = jnp.clip(t - w // 2, 0, T)
        hi = jnp.clip(t + w // 2, 0, T)
        ssum = csg[:, hi, g, :] - csg[:, lo, g, :]
        cnt = (hi - lo).astype(jnp.float32)[None, :, None]
        outs.append(ssum / cnt - ug[:, :, g, :])
    pooled = jnp.stack(outs, axis=2)
    y = jnp.einsum('btgc,gcd->btgd', pooled, pool_w.astype(jnp.float32)).reshape(B, T, C)
    return (y * pool_scale.astype(jnp.float32)).astype(u.dtype)


def _layer(x, ln1_g, w_in, pool_w, pool_scale, w_out, ln2_g, w1, w2):
    B, T, _ = x.shape
    h = _rmsnorm(x, ln1_g)
    proj = h @ w_in
    q = proj[..., :ATTN_WIDTH].reshape(B, T, N_HEADS, HEAD_DIM) * (HEAD_DIM ** -0.5)
    k = proj[..., ATTN_WIDTH:2 * ATTN_WIDTH].reshape(B, T, N_HEADS, HEAD_DIM)
    v = proj[..., 2 * ATTN_WIDTH:3 * ATTN_WIDTH].reshape(B, T, N_HEADS, HEAD_DIM)
    u = proj[..., 3 * ATTN_WIDTH:]
    attn = _dilated_attention(q, k, v).reshape(B, T, ATTN_WIDTH)
    pool = _multiscale_pool(u, pool_w, pool_scale)
    x = x + jnp.concatenate([attn, pool], axis=-1) @ w_out
    h = _rmsnorm(x, ln2_g)
    x = x + jnp.square(jax.nn.relu(h @ w1)) @ w2
    return x


def _trunk(x, ln1_g, w_in, pool_w, pool_scale, w_out, ln2_g, w1, w2, final_g):
    for l in range(DEPTH):
        x = _layer(x, ln1_g[l], w_in[l], pool_w[l], pool_scale[l], w_out[l], ln2_g[l], w1[l], w2[l])
    return _rmsnorm(x, final_g)


def setup_inputs(seed: int = 0) -> dict:
    key = jax.random.key(seed)
    ks = jax.random.split(key, 11)
    f32 = jnp.float32
    x_prompt = jax.random.normal(ks[0], (BATCH, SEQ, D_MODEL), f32)
    x_sample = jax.random.normal(ks[1], (DEC_BATCH, DEC_SEQ, D_MODEL), f32)
    ln1_g = 1.0 + 0.02 * jax.random.normal(ks[2], (DEPTH, D_MODEL), f32)
    w_in = jax.random.normal(ks[3], (DEPTH, D_MODEL, IN_WIDTH), f32) * D_MODEL ** -0.5
    pool_w = jax.random.normal(ks[4], (DEPTH, N_POOL_GROUPS, POOL_GROUP, POOL_GROUP), f32) * POOL_GROUP ** -0.5
    pool_scale = 1.0 + 0.1 * jax.random.normal(ks[5], (DEPTH, POOL_WIDTH), f32)
    w_out = jax.random.normal(ks[6], (DEPTH, D_MODEL, D_MODEL), f32) * D_MODEL ** -0.5
    ln2_g = 1.0 + 0.02 * jax.random.normal(ks[7], (DEPTH, D_MODEL), f32)
    w1 = jax.random.normal(ks[8], (DEPTH, D_MODEL, D_FF), f32) * D_MODEL ** -0.5
    w2 = jax.random.normal(ks[9], (DEPTH, D_FF, D_MODEL), f32) * D_FF ** -0.5
    final_g = 1.0 + 0.02 * jax.random.normal(ks[10], (D_MODEL,), f32)
    return {"x_prompt": x_prompt, "x_sample": x_sample, "ln1_g": ln1_g, "w_in": w_in,
            "pool_w": pool_w, "pool_scale": pool_scale, "w_out": w_out, "ln2_g": ln2_g,
            "w1": w1, "w2": w2, "final_g": final_g}


def reference(x_prompt, x_sample, ln1_g, w_in, pool_w, pool_scale, w_out, ln2_g, w1, w2, final_g):
    y_prompt = _trunk(x_prompt, ln1_g, w_in, pool_w, pool_scale, w_out, ln2_g, w1, w2, final_g)
    y_sample = _trunk(x_sample, ln1_g, w_in, pool_w, pool_scale, w_out, ln2_g, w1, w2, final_g)
    return (y_prompt, y_sample)
```

```python
import contextlib
import numpy as np
import concourse.bass as bass
import concourse.mybir as mybir
from concourse.bass_utils import run_bass_kernel_spmd

F32 = mybir.dt.float32
BF16 = mybir.dt.bfloat16
U8 = mybir.dt.uint8
AF = mybir.ActivationFunctionType
ALU = mybir.AluOpType

D = 2048
DFF = 8192
NREG = 7168
NOWN = 3072
EPS = 1e-6
NCORES = 8
ENGS = ("pe", "act", "dve", "pool", "sp")
EPOCH = 16000
WT_COLS = 640


class Op:
    __slots__ = ("eng", "fn", "reads", "writes", "dma", "deps", "marked", "sem", "val",
                 "idx", "semkey")

    def __init__(self, eng, fn, reads, writes, dma, semkey):
        self.eng, self.fn, self.reads, self.writes, self.dma = eng, fn, reads, writes, dma
        self.deps = []
        self.marked = False
        self.sem = None
        self.val = 0
        self.semkey = semkey


class Sched:
    def __init__(self, nc):
        self.nc = nc
        self.ops = []
        self.last_w = {}
        self.readers = {}
        self.last_eng = {}
        self.last_dma = {}
        self.pending_barrier = {}

    def barrier(self):
        snap = list(self.last_eng.values()) + list(self.last_dma.values())
        for e in ENGS:
            self.pending_barrier[e] = snap

    def op(self, eng, fn, reads=(), writes=(), dma=False, semkey=None):
        o = Op(eng, fn, tuple(reads), tuple(writes), dma, semkey)
        o.idx = len(self.ops)
        deps = set()
        for r in o.reads:
            w = self.last_w.get(r)
            if w is not None:
                deps.add(w)
        for w_ in o.writes:
            w = self.last_w.get(w_)
            if w is not None:
                deps.add(w)
            for rd in self.readers.get(w_, ()):
                deps.add(rd)
        for r in o.reads:
            self.readers.setdefault(r, []).append(o)
        for w_ in o.writes:
            self.last_w[w_] = o
            self.readers[w_] = []
        pb = self.pending_barrier.pop(eng, None)
        if pb:
            for d in pb:
                if d.eng == eng and not d.dma:
                    continue
                deps.add(d)
        deps.discard(o)
        for d in deps:
            if d.eng == "pe" and o.eng == "pe" and not d.dma and not o.dma:
                continue
            o.deps.append(d)
            d.marked = True
        self.ops.append(o)
        if dma:
            self.last_dma[semkey] = o
        else:
            self.last_eng[eng] = o
        return o

    def pe(self, fn, reads=(), writes=()):
        return self.op("pe", fn, reads, writes)

    def act(self, fn, reads=(), writes=()):
        return self.op("act", fn, reads, writes)

    def dve(self, fn, reads=(), writes=()):
        return self.op("dve", fn, reads, writes)

    def pool(self, fn, reads=(), writes=()):
        return self.op("pool", fn, reads, writes)

    def dma(self, eng, fn, reads=(), writes=(), semkey=None):
        if semkey is None:
            semkey = ("dma", eng, writes[0] if writes else reads[0])
        return self.op(eng, fn, reads, writes, dma=True, semkey=semkey)

    def emit(self):
        nc = self.nc
        stack = contextlib.ExitStack()
        sem_cache = {}

        def get_sem(key):
            if key not in sem_cache:
                sem_cache[key] = stack.enter_context(nc.semaphore("s%d" % len(sem_cache)))
            return sem_cache[key]

        eng_cnt = {e: 0 for e in ENGS}
        eng_epoch = {e: 0 for e in ENGS}
        dma_cnt = {}
        last_dma_on_key = {}
        for o in self.ops:
            if o.dma:
                k = o.semkey
                dma_cnt[k] = dma_cnt.get(k, 0) + 16
                o.sem = get_sem(k)
                o.val = dma_cnt[k]
                o.marked = True
                last_dma_on_key[k] = o
            elif o.marked:
                e = o.eng
                if eng_cnt[e] >= EPOCH:
                    eng_epoch[e] += 1
                    eng_cnt[e] = 0
                eng_cnt[e] += 1
                o.sem = get_sem(("eng", e, eng_epoch[e]))
                o.val = eng_cnt[e]
        self.n_sems = len(sem_cache)

        per_eng = {e: [] for e in ENGS}
        for o in self.ops:
            per_eng[o.eng].append(o)

        dma_issued = {}
        for o in self.ops:
            if o.dma:
                dma_issued.setdefault(o.semkey, []).append((o.idx, o.val))

        def dma_wait_val(dep, consumer_idx):
            return dep.val

        final_waits = [(o.sem, dma_cnt[k]) for k, o in last_dma_on_key.items()]

        def run_engine(ename, eng):
            waited = {}
            for o in per_eng[ename]:
                need = {}
                for d in o.deps:
                    v = dma_wait_val(d, o.idx) if d.dma else d.val
                    key = id(d.sem)
                    if key not in need or need[key][1] < v:
                        need[key] = (d.sem, v)
                for key, (sem, v) in need.items():
                    if waited.get(key, 0) >= v:
                        continue
                    eng.wait_ge(sem, v)
                    waited[key] = v
                ins = o.fn(eng)
                if o.marked:
                    ins.then_inc(o.sem, 16 if o.dma else 1)
            if ename == "sp":
                for (sem, v) in final_waits:
                    eng.wait_ge(sem, v)

        with nc.Block() as block:
            @block.tensor
            def _(e):
                run_engine("pe", e)

            @block.scalar
            def _(e):
                run_engine("act", e)

            @block.vector
            def _(e):
                run_engine("dve", e)

            @block.gpsimd
            def _(e):
                run_engine("pool", e)

            @block.sync
            def _(e):
                run_engine("sp", e)
        stack.close()


_DTSIZE = {F32: 4, BF16: 2}


class Arena:
    def __init__(self, t):
        self.t = t

    def view(self, off, shape, dt):
        n = 1
        for s in shape[1:]:
            n *= s
        nbytes = n * _DTSIZE[dt]
        assert off % 4 == 0 and off + nbytes <= self.t.shape[1], (off, nbytes, self.t.shape)
        ap = self.t[:, off:off + nbytes].bitcast(dt)
        if len(shape) == 3:
            ap = ap.rearrange("p (a b) -> p a b", b=shape[2])
        elif len(shape) == 4:
            ap = ap.rearrange("p (a b c) -> p a b c", b=shape[2], c=shape[3])
        return ap


def bcast_mid(ap2d, reps):
    a = ap2d.ap
    return bass.AP(ap2d.tensor, ap2d.offset, [list(a[0]), [0, reps], list(a[1])])


KB = 1024


def ss(start, n, step):
    return slice(start, start + (n - 1) * step + 1, step)


def build_program():
    nc = bass.Bass("TRN2", target_bir_lowering=False)

    def din(name, shape, dt=F32):
        return nc.dram_tensor(name, list(shape), dt, kind="ExternalInput").ap()

    xr = din("xr", [NREG, D])
    vmask_d = din("vmask", [128, 56])
    rcnt_d = din("rcnt", [128, 4 * NOWN])
    wtab_d = din("wtab", [128, 16 * WT_COLS])
    g1_d = din("g1", [128, D])
    g2_d = din("g2", [128, D])
    gf_d = din("gf", [128, D])
    w_in = din("w_in", [D, 4096])
    w_out = din("w_out", [D, D])
    w1 = din("w1", [D, DFF])
    w2 = din("w2", [DFF, D])
    pool_w = din("pool_w", [4, 256, 256])
    pscale_d = din("pscale", [128, 8])
    ident_d = din("ident", [128, 128])
    rcntB_d = din("rcntB", [128, 4 * 1024])
    y = nc.dram_tensor("y", [NOWN, D], F32, kind="ExternalOutput").ap()
    KT = nc.dram_tensor("KT", [1024, NREG], BF16).ap()
    VA = nc.dram_tensor("VA", [NREG, 2048], BF16).ap()
    QT = nc.dram_tensor("QT", [1024, NOWN], BF16).ap()
    UT = nc.dram_tensor("UT", [1024, NREG], F32).ap()
    WTB = nc.dram_tensor("WTB", [128, 16 * WT_COLS], BF16).ap()
    PWB = nc.dram_tensor("PWB", [4, 256, 256], BF16).ap()
    WOB = nc.dram_tensor("WOB", [D, D], BF16).ap()

    S = Sched(nc)
    stack = contextlib.ExitStack()
    ARENA_BYTES = 200 * KB
    arena_t = stack.enter_context(nc.sbuf_tensor("arena", [128, ARENA_BYTES], U8))
    A = Arena(arena_t)
    ident = stack.enter_context(nc.sbuf_tensor("ident_sb", [128, 128], BF16))
    ones3 = stack.enter_context(nc.sbuf_tensor("ones3", [128, 16, 64], BF16))
    vm = stack.enter_context(nc.sbuf_tensor("vm_sb", [128, 56], F32))
    pscale = stack.enter_context(nc.sbuf_tensor("pscale_sb", [128, 8], F32))
    ssq = stack.enter_context(nc.sbuf_tensor("ssq", [128, 2, 8], F32))
    rstd = stack.enter_context(nc.sbuf_tensor("rstd", [128, 2, 8], F32))
    ps = stack.enter_context(nc.psum_tensor("ps", [128, 4096], F32))

    def bank(b):
        return ps[:, 512 * b:512 * b + 512]

    def bankbf(b):
        return ps[:, 512 * b:512 * b + 512].bitcast(BF16)

    bank_rr = [0]

    def next_bank(lo=0, hi=8):
        b = lo + bank_rr[0] % (hi - lo)
        bank_rr[0] += 1
        return b

    evac_rr = [0]

    def evac_copy(out, in_, reads, writes, scale=None):
        evac_rr[0] += 1
        if evac_rr[0] % 2 == 0:
            if scale is None:
                S.act(lambda e: e.activation(out=out, in_=in_, func=AF.Copy), reads, writes)
            else:
                S.act(lambda e: e.activation(out=out, in_=in_, func=AF.Copy, scale=scale), reads, writes)
        else:
            if scale is None:
                S.dve(lambda e: e.tensor_copy(out=out, in_=in_), reads, writes)
            else:
                S.dve(lambda e: e.tensor_scalar(out=out, in0=in_, scalar1=scale, scalar2=None,
                                                op0=ALU.mult), reads, writes)

    S.dma("pool", lambda e: e.dma_start(out=ident[:], in_=ident_d[:]), writes=["ident"])
    S.dma("sp", lambda e: e.dma_start(out=vm[:], in_=vmask_d[:]), writes=["vm"])
    S.dma("sp", lambda e: e.dma_start(out=pscale[:], in_=pscale_d[:]), writes=["pscale"])
    S.pool(lambda e: e.memset(ones3[:], 1.0), writes=["ones3"])
    S.dma("pool", lambda e: e.dma_start(out=WTB[:], in_=wtab_d[:]), writes=["WTB"])
    S.dma("pool", lambda e: e.dma_start(out=PWB[:], in_=pool_w[:]), writes=["PWB"])

    def rms_stats(src_tiles, slot, n, reads, junk):
        S.dve(lambda e: e.memset(ssq[:, slot, 0:n], 0.0), writes=[("ssq", slot, j) for j in range(n)])
        for j in range(n):
            S.act(lambda e, j=j: e.activation(out=junk, in_=src_tiles(j), func=AF.Square,
                                              accum_out=ssq[:, slot, j:j + 1]),
                  reads=reads(j), writes=[("ssq", slot, j), "junk"])
        S.dve(lambda e: e.tensor_scalar(out=rstd[:, slot, 0:n], in0=ssq[:, slot, 0:n],
                                        scalar1=1.0 / D, scalar2=EPS, op0=ALU.mult, op1=ALU.add),
              reads=[("ssq", slot, j) for j in range(n)], writes=[("rstd", slot)])
        S.act(lambda e: e.activation(out=rstd[:, slot, 0:n], in_=rstd[:, slot, 0:n], func=AF.Ln),
              reads=[("rstd", slot)], writes=[("rstd", slot)])
        S.act(lambda e: e.activation(out=rstd[:, slot, 0:n], in_=rstd[:, slot, 0:n], func=AF.Exp, scale=-0.5),
              reads=[("rstd", slot)], writes=[("rstd", slot)])

    wA = A.view(0, [128, 16, 2048], BF16)
    xt = A.view(64 * KB, [128, 4, 2048], F32)
    hb = A.view(96 * KB, [128, 4, 2048], BF16)
    hT = [A.view(112 * KB + 16 * KB * i, [128, 16, 512], BF16) for i in range(2)]
    st1 = A.view(144 * KB, [128, 8, 512], BF16)
    st2_va = A.view(152 * KB, [128, 4, 16, 128], BF16)
    st2_va_flat = A.view(152 * KB, [128, 4, 2048], BF16)
    st2_u = A.view(180 * KB, [128, 8, 512], F32)
    g1 = A.view(168 * KB, [128, 2048], F32)
    junkA = A.view(176 * KB, [128, 2048], BF16)

    S.dma("sp", lambda e: e.dma_start(out=g1, in_=g1_d[:]), writes=["g1"])
    w_in_r = w_in.rearrange("(kc p) n -> p kc n", p=128)
    KT_r = KT.rearrange("(m p) t -> p m t", p=128)
    QT_r = QT.rearrange("(m p) t -> p m t", p=128)
    UT_r = UT.rearrange("(m p) t -> p m t", p=128)
    VA_r = VA.rearrange("(j p) c -> p j c", p=128)
    xr_r = xr.rearrange("(j p) f -> p j f", p=128)

    def load_wA(colsets):
        for (dc, sc, n) in colsets:
            for cg in range(n // 512):
                S.dma("pool", lambda e, dc=dc, sc=sc, cg=cg: e.dma_start(
                    out=wA[:, :, dc + 512 * cg:dc + 512 * cg + 512],
                    in_=w_in_r[:, :, sc + 512 * cg:sc + 512 * cg + 512]),
                    writes=[("wA", dc // 1024, cg)], semkey=("wA", dc // 1024, cg))

    def a_load(row0, nt):
        S.dma("sp", lambda e: e.dma_start(out=xt[:, 0:nt, :], in_=xr_r[:, row0 // 128:row0 // 128 + nt, :]),
              writes=["xt"])

    def a_sq(k, nt):
        rms_stats(lambda j: xt[:, j, :], k % 2, nt, lambda j: ["xt"], junkA)

    def a_stt(k, nt):
        slot = k % 2
        for j in range(nt):
            S.dve(lambda e, j=j: e.scalar_tensor_tensor(out=hb[:, j, :], in0=xt[:, j, :],
                                                        scalar=rstd[:, slot, j:j + 1], in1=g1,
                                                        op0=ALU.mult, op1=ALU.mult),
                  reads=["xt", ("rstd", slot), "g1"], writes=[("hb", j)])

    def a_T(k, nt):
        slot = k % 2
        for kp in range(8):
            tb = next_bank(0, 3)
            for kk in range(2):
                for j in range(nt):
                    kc = 2 * kp + kk
                    S.pe(lambda e, tb=tb, kk=kk, j=j, kc=kc: e.transpose(
                        out=bankbf(tb)[:, kk * 512 + j * 128:kk * 512 + j * 128 + 128],
                        in_=hb[:, j, kc * 128:kc * 128 + 128], identity=ident[:]),
                        reads=[("hb", j), "ident"], writes=[("ps", tb)])
            dst = hT[slot][:, 2 * kp:2 * kp + 2, 0:128 * nt]
            evac_copy(dst, bankbf(tb).rearrange("p (a b) -> p a b", b=512)[:, :, 0:128 * nt],
                      reads=[("ps", tb)], writes=[("hT", slot, kp)])

    def run_items(items):
        n = len(items)
        ld = lambda i: a_load(items[i][0], items[i][1])
        sq = lambda i: a_sq(i, items[i][1])
        st = lambda i: a_stt(i, items[i][1])
        tr = lambda i: a_T(i, items[i][1])
        ld(0); sq(0); st(0)
        if n > 1:
            ld(1)
        tr(0)
        if n > 1:
            sq(1); st(1)
        if n > 2:
            ld(2); sq(2)
        for i in range(n):
            row0, nt, fa, fb, pre = items[i]
            if pre is not None:
                pre()
            fa(row0, i % 2)
            if i + 1 < n:
                tr(i + 1)
            if i + 2 < n:
                st(i + 2)
            if i + 3 < n:
                ld(i + 3)
            fb(row0, i % 2)
            if i + 3 < n:
                sq(i + 3)

    def hT_reads(slot):
        return [("hT", slot, kp) for kp in range(8)]

    def proj_fm(slot, wcol0, stage, scale, dst_r, dst_t0, name, ntok=512):
        for m in range(8):
            pb = next_bank(3, 8)
            for kc in range(16):
                S.pe(lambda e, pb=pb, kc=kc, m=m: e.matmul(
                    bank(pb)[:, 0:ntok], lhsT=wA[:, kc, wcol0 + m * 128:wcol0 + m * 128 + 128],
                    rhs=hT[slot][:, kc, 0:ntok], start=(kc == 0), stop=(kc == 15)),
                    reads=hT_reads(slot) + [("wA", wcol0 // 1024, m // 4)], writes=[("ps", pb)])
            evac_copy(stage[:, m, 0:ntok], bank(pb)[:, 0:ntok], reads=[("ps", pb)], writes=[(name, m)],
                      scale=scale)
        S.dma("sp", lambda e: e.dma_start(out=dst_r[:, :, dst_t0:dst_t0 + ntok], in_=stage[:, :, 0:ntok]),
              reads=[(name, m) for m in range(8)], writes=[(name + "_d", dst_t0)], semkey=("out", name))

    def proj_v(row0, slot):
        b = row0 // 512
        for j in range(4):
            for n in range(2):
                pb = next_bank(3, 8)
                for kc in range(16):
                    S.pe(lambda e, pb=pb, kc=kc, j=j, n=n: e.matmul(
                        bank(pb), lhsT=hT[slot][:, kc, j * 128:j * 128 + 128],
                        rhs=wA[:, kc, 1024 + n * 512:1024 + n * 512 + 512],
                        start=(kc == 0), stop=(kc == 15)),
                        reads=hT_reads(slot) + [("wA", 1, n)], writes=[("ps", pb)])
                evac_copy(st2_va[:, j, 8 * n:8 * n + 8, 0:64],
                          bank(pb).rearrange("p (a b) -> p a b", b=64),
                          reads=[("ps", pb)], writes=[("st2", j, n)])
            S.act(lambda e, j=j: e.activation(out=st2_va[:, j, :, 64:128], in_=ones3[:], func=AF.Copy,
                                              scale=vm[:, 4 * b + j:4 * b + j + 1]),
                  reads=["ones3", "vm"], writes=[("st2v", j)])
        S.dma("sp", lambda e: e.dma_start(out=VA_r[:, 4 * b:4 * b + 4, :], in_=st2_va_flat),
              reads=[("st2", j, n) for j in range(4) for n in range(2)] + [("st2v", j) for j in range(4)],
              writes=[("VA_d", b)], semkey=("out", "VA"))

    own_off = {2: 0, 3: 512, 4: 1024, 5: 1536, 10: 2048, 11: 2560}
    nop = lambda row0, sl: None
    fa1 = lambda row0, sl: proj_fm(sl, 0, st1, None, KT_r, row0, "st1")
    fb1 = lambda row0, sl: proj_v(row0, sl)
    fa2 = lambda row0, sl: proj_fm(sl, 0, st2_u, None, UT_r, row0, "st2u")
    fa2m = lambda row0, sl: proj_fm(sl, 0, st2_u, None, UT_r, row0, "st2u", ntok=128)
    fb2 = lambda row0, sl: proj_fm(sl, 1024, st1, 0.125, QT_r, own_off[row0 // 512], "st1")
    items = [(512 * b, 4, fa1, fb1, None) for b in range(14) if b not in (8, 9)]
    S.dve(lambda e: e.memset(hT[1].rearrange("p a b -> p (a b)"), 0.0), writes=[("hT", 1, kp) for kp in range(8)])
    zsrc = hT[1].rearrange("p a b -> p (a b)")
    S.dma("sp", lambda e: e.dma_start(out=KT_r[:, :, 4096:5120],
                                      in_=zsrc.rearrange("p (m t) -> p m t", t=1024)),
          reads=[("hT", 1, kp) for kp in range(8)], writes=[("zKT", 0)])
    for hh in range(2):
        S.dma("sp", lambda e, hh=hh: e.dma_start(out=VA_r[:, 32 + 4 * hh:32 + 4 * hh + 4, :],
                                                 in_=zsrc.rearrange("p (j c) -> p j c", c=2048)),
              reads=[("hT", 1, kp) for kp in range(8)], writes=[("zVA", hh)])
    pass2 = [(1024, 4, fa2, fb2), (896, 1, fa2m, nop), (1536, 4, fa2, fb2), (3072, 1, fa2m, nop),
             (2048, 4, fa2, fb2), (4992, 1, fa2m, nop), (2560, 4, fa2, fb2), (6144, 1, fa2m, nop),
             (5120, 4, fa2, fb2), (5632, 4, fa2, fb2)]
    for i_, (row0, nt, fa, fb) in enumerate(pass2):
        pre = (lambda: load_wA([(0, 3072, 1024), (1024, 0, 1024)])) if i_ == 0 else None
        items.append((row0, nt, fa, fb, pre))
    for q_, ii in enumerate((3, 5, 7, 9)):
        it = items[ii]
        items[ii] = (it[0], it[1], it[2], it[3],
                     lambda q_=q_: S.dma("pool", lambda e: e.dma_start(
                         out=WOB[512 * q_:512 * q_ + 512, :], in_=w_out[512 * q_:512 * q_ + 512, :]),
                         reads=[("st1", 7)], writes=[("WOB", q_)], semkey=("WOB", q_)))
    load_wA([(0, 1024, 1024), (1024, 2048, 1024)])
    run_items(items)

    actT = A.view(0, [128, 16, 1024], BF16)
    g2 = A.view(32 * KB, [128, 2048], F32)
    gf = A.view(40 * KB, [128, 2048], F32)
    junkB = A.view(48 * KB, [128, 2048], BF16)
    BA = 52 * KB
    o = BA
    qt = [A.view(o + 2 * KB * i, [128, 1024], BF16) for i in range(2)]; o += 4 * KB
    kt = [A.view(o + 6 * KB * i, [128, 3072], BF16) for i in range(2)]; o += 12 * KB
    va1 = A.view(o, [128, 9, 256], BF16); o += 9 * 512
    va4 = A.view(o, [128, 4, 3, 256], BF16); o += 6 * KB
    va16a = A.view(o, [128, 16, 256], BF16); o += 8 * KB
    va16b = A.view(o, [128, 16, 256], BF16); o += 8 * KB
    wt = A.view(o, [128, 16, WT_COLS], BF16); o += 20 * KB
    NSL = 8
    et = [A.view(o + KB * i, [128, 512], BF16) for i in range(NSL)]; o += 8 * KB
    pt = [A.view(o + KB * i, [128, 512], BF16) for i in range(NSL)]; o += 8 * KB
    rc = [A.view(o + 4 * KB * i, [128, 1024], F32) for i in range(2)]; o += 8 * KB
    kt16 = A.view(o, [128, 16, 192], BF16); o += 6 * KB
    kt4 = A.view(o, [128, 4, 768], BF16); o += 6 * KB
    qt16 = A.view(o, [128, 16, 64], BF16); o += 2 * KB
    qt4 = A.view(o, [128, 4, 256], BF16); o += 2 * KB
    UTW = 1040
    ut = [A.view(o + 4160 * i, [128, UTW], F32) for i in range(2)]; o += 2 * 4160
    sa = A.view(o, [128, UTW], F32); o += 4160
    sb = A.view(o, [128, UTW], F32); o += 4160
    rcg = [A.view(o + 4 * KB * i, [128, 1024], F32) for i in range(2)]; o += 8 * KB
    pooledT = A.view(o, [128, 8, 1024], BF16); o += 16 * KB
    pw = A.view(o, [128, 4, 2, 256], BF16); o += 4 * KB
    assert o <= ARENA_BYTES, o
    x1 = A.view(BA, [128, 8, 2048], F32)
    wbA = [A.view(BA + 64 * KB + 16 * KB * i, [128, 16, 512], BF16) for i in range(2)]
    wbB = [A.view(BA + 96 * KB + 16 * KB * i, [128, 4, 2048], BF16) for i in range(2)]
    wbB_o = [A.view(BA + 96 * KB + 16 * KB * i, [128, 16, 512], BF16) for i in range(2)]
    hb2 = [A.view(BA + 128 * KB + 4 * KB * i, [128, 2048], BF16) for i in range(4)]
    aT = [A.view(BA + 128 * KB + 8 * KB * i, [128, 4, 1024], BF16) for i in range(2)]
    rl = [A.view(BA + 144 * KB + 2 * KB * i, [128, 512], F32) for i in range(2)]
    assert BA + 148 * KB <= ARENA_BYTES

    WOB_r = WOB.rearrange("(kc p) n -> p kc n", p=128)
    w1_r = w1.rearrange("(kc p) n -> p kc n", p=128)
    w2_r = w2.rearrange("(g fc p) n -> p g fc n", p=128, fc=4)
    pool_w_r = PWB.rearrange("g (cc p) d -> p g cc d", p=128)
    y_r = y.rearrange("(j p) f -> p j f", p=128)
    wb_use = [0]
    has_pool_tt = hasattr(nc.gpsimd, "tensor_tensor")

    def ew_tt(fn_args, reads, writes):
        if has_pool_tt:
            S.pool(lambda e: e.tensor_tensor(**fn_args), reads, writes)
        else:
            S.dve(lambda e: e.tensor_tensor(**fn_args), reads, writes)

    def ew_dve(fn_args, reads, writes):
        S.dve(lambda e: e.tensor_tensor(**fn_args), reads, writes)

    def do_unit(u, R0):
        O0 = 1024 * u
        S.barrier()
        if u == 0:
            S.dma("sp", lambda e: e.dma_start(out=g2, in_=g2_d[:]), writes=["g2"])
            S.dma("sp", lambda e: e.dma_start(out=gf, in_=gf_d[:]), writes=["gf"])
        S.dma("sp", lambda e: e.dma_start(out=wt.rearrange("p a b -> p (a b)"), in_=WTB[:]),
              reads=["WTB"], writes=["wt"])

        batches = []
        head_ctr = 0
        def load_qk(hp):
            sl = hp % 2
            S.dma("sp", lambda e: e.dma_start(
                out=qt[sl], in_=QT[hp * 128:hp * 128 + 128, O0:O0 + 1024]), writes=[("qt", sl)])
            S.dma("sp", lambda e: e.dma_start(
                out=kt[sl], in_=KT[hp * 128:hp * 128 + 128, R0:R0 + 3072]), writes=[("kt", sl)])

        def va_src(hp, row0, pat):
            return bass.AP(VA.tensor, VA.offset + row0 * 2048 + hp * 256, pat)

        def load_va1(hp):
            S.dma("sp", lambda e, s=va_src(hp, R0 + 960, [[2048, 128], [128 * 2048, 9], [1, 256]]): e.dma_start(
                out=va1, in_=s), writes=["va1"])

        def load_va4(hp):
            for c in range(4):
                S.dma("sp", lambda e, c=c, s=va_src(hp, R0 + 768 + c, [[4 * 2048, 128], [512 * 2048, 3], [1, 256]]):
                      e.dma_start(out=va4[:, c, :, :], in_=s), writes=[("va4", c)])

        def load_va16(hp):
            S.dma("sp", lambda e, s=va_src(hp, R0, [[16 * 2048, 128], [2048, 16], [1, 256]]): e.dma_start(
                out=va16a, in_=s), writes=["va16a"])
            S.dma("sp", lambda e, s=va_src(hp, R0 + 2048, [[16 * 2048, 64], [2048, 16], [1, 256]]): e.dma_start(
                out=va16b[0:64, :, :], in_=s), writes=["va16b"])

        def copy4(hp):
            sl = hp % 2
            S.dve(lambda e: e.tensor_copy(out=kt4, in_=kt[sl].rearrange("p (i c) -> p c i", c=4)),
                  reads=[("kt", sl)], writes=["kt4"])
            S.act(lambda e: e.activation(out=qt4, in_=qt[sl].rearrange("p (i c) -> p c i", c=4),
                                         func=AF.Copy), reads=[("qt", sl)], writes=["qt4"])

        def copy16(hp):
            sl = hp % 2
            S.dve(lambda e: e.tensor_copy(out=kt16, in_=kt[sl].rearrange("p (i c) -> p c i", c=16)),
                  reads=[("kt", sl)], writes=["kt16"])
            S.act(lambda e: e.activation(out=qt16, in_=qt[sl].rearrange("p (i c) -> p c i", c=16),
                                         func=AF.Copy), reads=[("qt", sl)], writes=["qt16"])

        pre_qk = {}
        post_pv = {}
        post_qk = {}
        for hp in range(8):
            if hp + 1 < 8:
                pre_qk.setdefault(24 * hp, []).append(lambda hp=hp: load_qk(hp + 1))
                post_pv.setdefault(24 * hp + 15, []).append(lambda hp=hp: load_va1(hp + 1))
                post_pv.setdefault(24 * hp + 19, []).append(lambda hp=hp: load_va4(hp + 1))
                post_pv.setdefault(24 * hp + 23, []).append(lambda hp=hp: load_va16(hp + 1))
                post_qk.setdefault(24 * hp + 19, []).append(lambda hp=hp: copy4(hp + 1))
                post_qk.setdefault(24 * hp + 23, []).append(lambda hp=hp: copy16(hp + 1))
        load_qk(0)
        copy4(0)
        copy16(0)
        load_va1(0)
        load_va4(0)
        load_va16(0)
        S.dma("sp", lambda e: e.dma_start(out=pw, in_=pool_w_r), reads=["PWB"], writes=["pw"])
        def pool_tile(m):
            g = m // 2
            us = m % 2
            S.dma("pool", lambda e, m=m, us=us: e.dma_start(
                out=ut[us], in_=UT[m * 128:m * 128 + 128, R0 + 1024 - 8:R0 + 1024 - 8 + UTW]),
                writes=[("ut", us)])
            if m % 2 == 0:
                if u == 2:
                    S.dma("pool", lambda e, g=g: e.dma_start(
                        out=rcg[0], in_=rcnt_d[:, g * NOWN + O0:g * NOWN + O0 + 1024]), writes=[("rcg", 0)])
                    S.dma("pool", lambda e, g=g: e.dma_start(
                        out=rcg[1], in_=rcntB_d[:, g * 1024:g * 1024 + 1024]), writes=[("rcg", 1)])
                else:
                    gs = g % 2
                    S.dma("pool", lambda e, g=g, gs=gs: e.dma_start(
                        out=rcg[gs], in_=rcnt_d[:, g * NOWN + O0:g * NOWN + O0 + 1024]), writes=[("rcg", gs)])
            uu = ut[us]
            ew_dve(dict(out=sa[:, 1:1040], in0=uu[:, 0:1039], in1=uu[:, 1:1040], op=ALU.add),
                  [("ut", us)], ["sa"])
            cur, oth, curname, othname = sa, sb, "sa", "sb"
            lo, hi = 1, 1040
            sh = 1
            for lvl in range(g):
                nlo, nhi = lo + sh, hi - sh
                ew_dve(dict(out=oth[:, nlo:nhi], in0=cur[:, nlo - sh:nhi - sh], in1=cur[:, nlo + sh:nhi + sh],
                           op=ALU.add), [curname], [othname])
                cur, oth, curname, othname = oth, cur, othname, curname
                lo, hi = nlo, nhi
                sh *= 2
            if u == 2:
                ew_dve(dict(out=oth[:, 8:1032], in0=cur[:, 8:1032], in1=rcg[0], op=ALU.mult),
                      [curname, ("rcg", 0)], [othname])
                ew_dve(dict(out=oth[:, 8:1032], in0=oth[:, 8:1032], in1=uu[:, 8:1032], op=ALU.subtract),
                      [othname, ("ut", us)], [othname])
                ew_dve(dict(out=cur[:, 9:1033], in0=cur[:, 9:1033], in1=rcg[1], op=ALU.mult),
                      [curname, ("rcg", 1)], [curname])
                ew_dve(dict(out=pooledT[:, m, :], in0=oth[:, 8:1032], in1=cur[:, 9:1033], op=ALU.add),
                      [othname, curname], [("pooledT", m)])
            else:
                gs = g % 2
                ew_dve(dict(out=oth[:, 8:1032], in0=cur[:, 8:1032], in1=rcg[gs], op=ALU.mult),
                      [curname, ("rcg", gs)], [othname])
                ew_dve(dict(out=pooledT[:, m, :], in0=oth[:, 8:1032], in1=uu[:, 8:1032], op=ALU.subtract),
                      [othname, ("ut", us)], [("pooledT", m)])
        for m_ in range(8):
            post_qk.setdefault(4 + 20 * m_, []).append(lambda m_=m_: pool_tile(m_))
        for hp in range(8):
            sl = hp % 2
            for hl in range(2):
                h = 2 * hp + hl
                hbp = 64 * hl
                aset = head_ctr % 2
                head_ctr += 1
                accb = 1024 * aset
                first = {0: True, 1: True}
                vcol = hl * 128
                qv = qt[sl]
                kv = kt[sl]
                hb_list = []
                for bb in range(4):
                    tiles = []
                    for bi in range(2):
                        b = 2 * bb + bi
                        q_ap = qv[hbp:hbp + 64, 128 * b:128 * b + 128]
                        for ti in range(2):
                            k = b + ti
                            k_ap = kv[hbp:hbp + 64, 960 + 128 * k:960 + 128 * k + 128]
                            v_ap = va1[:, k, vcol:vcol + 128]
                            outs = [(b // 4, ps[:, accb + 128 * b:accb + 128 * b + 128], 0, 128)]
                            tiles.append(dict(slot=(2 * bi + ti) * 128, nq=128, nk=128, k=k_ap, q=q_ap,
                                              v=v_ap, outs=outs, vres="va1", qkres=[("kt", sl), ("qt", sl)]))
                    tiles = [tiles[0], tiles[2], tiles[1], tiles[3]]
                    hb_list.append(dict(tiles=tiles, w=wt[:, h, 0:256], wrep=2, wn=256))
                for c in range(4):
                    tiles = []
                    for b in range(2):
                        q_ap = qt4[hbp:hbp + 64, c, 128 * b:128 * b + 128]
                        for ti in range(2):
                            k = b + ti
                            k_ap = kt4[hbp:hbp + 64, c, 192 + 128 * k:192 + 128 * k + 128]
                            v_ap = va4[:, c, k, vcol:vcol + 128]
                            outs = [(b, ps[:, ss(accb + 512 * b + c, 128, 4)], 0, 128)]
                            tiles.append(dict(slot=(2 * b + ti) * 128, nq=128, nk=128, k=k_ap, q=q_ap,
                                              v=v_ap, outs=outs, vres=("va4", c), qkres=["kt4", "qt4"]))
                    tiles = [tiles[0], tiles[2], tiles[1], tiles[3]]
                    hb_list.append(dict(tiles=tiles, w=wt[:, h, 256:512], wrep=2, wn=256))
                for cg in range(4):
                    tiles = []
                    for ci in range(4):
                        c = 4 * cg + ci
                        q_ap = qt16[hbp:hbp + 64, c, :]
                        for ti in range(2):
                            if ti == 0:
                                nk = 128
                                k_ap = kt16[hbp:hbp + 64, c, 0:128]
                                v_ap = va16a[:, c, vcol:vcol + 128]
                                vres = "va16a"
                            else:
                                nk = 64
                                k_ap = kt16[hbp:hbp + 64, c, 128:192]
                                v_ap = va16b[0:64, c, vcol:vcol + 128]
                                vres = "va16b"
                            outs = [(q2, ps[:, ss(accb + 512 * q2 + c, 32, 16)], 32 * q2, 32)
                                    for q2 in range(2)]
                            tiles.append(dict(slot=(2 * ci + ti) * 64, nq=64, nk=nk, k=k_ap, q=q_ap,
                                              v=v_ap, outs=outs, vres=vres, qkres=["kt16", "qt16"]))
                    tiles = [t_ for t_ in tiles if t_["nk"] == 128] + [t_ for t_ in tiles if t_["nk"] == 64]
                    hb_list.append(dict(tiles=tiles, w=wt[:, h, 512:640], wrep=4, wn=128))
                for bi_, bt in enumerate(hb_list):
                    bt.update(hp=hp, hl=hl, aset=aset, sl=sl, first=first, accb=accb, hbp=hbp,
                              last=(bi_ == len(hb_list) - 1))
                    batches.append(bt)

        LAG = 3
        nb = len(batches)

        def emit_qk(i):
            bt = batches[i]
            sbk = 4 + i % 4
            sl = bt["sl"]
            for t in bt["tiles"]:
                S.pe(lambda e, t=t, sbk=sbk: e.matmul(
                    ps[0:t["nk"], 512 * sbk + t["slot"]:512 * sbk + t["slot"] + t["nq"]],
                    lhsT=t["k"], rhs=t["q"], start=True, stop=True, skip_group_check=True),
                    reads=t["qkres"], writes=[("ps", sbk)])
            ei = i % NSL
            S.act(lambda e, sbk=sbk, ei=ei: e.activation(out=et[ei], in_=bank(sbk), func=AF.Exp),
                  reads=[("ps", sbk)], writes=[("et", ei)])
            wn, wrep = bt["wn"], bt["wrep"]
            S.dve(lambda e, ei=ei, bt=bt, wn=wn, wrep=wrep: e.tensor_tensor(
                out=pt[ei].rearrange("p (a b) -> p a b", b=wn),
                in0=et[ei].rearrange("p (a b) -> p a b", b=wn),
                in1=bcast_mid(bt["w"], wrep), op=ALU.mult),
                reads=[("et", ei), "wt"], writes=[("pt", ei)])

        def emit_pv(i):
            bt = batches[i]
            ei = i % NSL
            aset = bt["aset"]
            for t in bt["tiles"]:
                for (half, o_ap, poff, n) in t["outs"]:
                    bk = 2 * aset + half
                    st = bt["first"][half]
                    bt["first"][half] = False
                    S.pe(lambda e, t=t, o_ap=o_ap, poff=poff, n=n, st=st, ei=ei: e.matmul(
                        o_ap, lhsT=t["v"],
                        rhs=pt[ei][0:t["nk"], t["slot"] + poff:t["slot"] + poff + n],
                        start=st, stop=True, skip_group_check=True),
                        reads=[("pt", ei), t["vres"]], writes=[("ps", bk)])
            if bt["last"]:
                hp, hbp, accb = bt["hp"], bt["hbp"], bt["accb"]
                S.act(lambda e, aset=aset, accb=accb: e.activation(
                    out=rc[aset][64:128, :], in_=ps[64:128, accb:accb + 1024], func=AF.Ln),
                    reads=[("ps", 2 * aset), ("ps", 2 * aset + 1)], writes=[("rc", aset)])
                S.act(lambda e, aset=aset: e.activation(
                    out=rc[aset][64:128, :], in_=rc[aset][64:128, :], func=AF.Exp, scale=-1.0),
                    reads=[("rc", aset)], writes=[("rc", aset)])
                S.dve(lambda e, aset=aset, accb=accb, hp=hp, hbp=hbp: e.tensor_tensor(
                    out=actT[hbp:hbp + 64, hp, :], in0=ps[0:64, accb:accb + 1024],
                    in1=rc[aset][64:128, :], op=ALU.mult),
                    reads=[("ps", 2 * aset), ("ps", 2 * aset + 1), ("rc", aset)],
                    writes=[("actT", hp, hbp)])

        def step_qk(i):
            if i < nb:
                for fn in pre_qk.get(i, ()):
                    fn()
                emit_qk(i)
                for fn in post_qk.get(i, ()):
                    fn()

        def step_pv(i):
            if 0 <= i < nb:
                emit_pv(i)
                for fn in post_pv.get(i, ()):
                    fn()

        for i in range(0, nb + 4, 4):
            for k_ in range(4):
                step_qk(i + k_)
            for k_ in range(4):
                step_pv(i - 4 + k_)

        for g in range(4):
            for dtl in range(2):
                for half in range(2):
                    pb = next_bank(0, 8)
                    for cc in range(2):
                        S.pe(lambda e, pb=pb, g=g, cc=cc, dtl=dtl, half=half: e.matmul(
                            bank(pb), lhsT=pw[:, g, cc, dtl * 128:dtl * 128 + 128],
                            rhs=pooledT[:, 2 * g + cc, half * 512:half * 512 + 512],
                            start=(cc == 0), stop=(cc == 1)),
                            reads=["pw", ("pooledT", 2 * g), ("pooledT", 2 * g + 1)], writes=[("ps", pb)])
                    mo = 2 * g + dtl
                    S.act(lambda e, pb=pb, mo=mo, half=half: e.activation(
                        out=actT[:, 8 + mo, half * 512:half * 512 + 512], in_=bank(pb), func=AF.Copy,
                        scale=pscale[:, mo:mo + 1]),
                        reads=[("ps", pb), "pscale"], writes=[("actT", 8 + mo, half)])

        S.barrier()
        actT_all = [("actT", hp, hbp) for hp in range(8) for hbp in (0, 64)] + \
                   [("actT", 8 + mo, half) for mo in range(8) for half in range(2)]
        wo = [wbA[0], wbA[1], wbB_o[0], wbB_o[1]]
        wo_res = [("wbA", 0), ("wbA", 1), ("wbB", 0), ("wbB", 1)]

        def load_wo(n):
            S.dma("sp", lambda e: e.dma_start(out=wo[n], in_=WOB_r[:, :, n * 512:n * 512 + 512]),
                  reads=[("WOB", q_) for q_ in range(4)], writes=[wo_res[n]], semkey=("wo", n))

        def load_x1(j):
            S.dma("sp", lambda e: e.dma_start(
                out=x1[:, j, :], in_=xr_r[:, (R0 + 1024) // 128 + j, :]),
                writes=[("x1", j)], semkey=("x1", j))

        load_wo(0)
        load_x1(0)
        load_x1(1)
        load_wo(1)
        for j in range(2, 8):
            load_x1(j)
        load_wo(2)
        load_wo(3)

        def oproj(j, n):
            pb = next_bank(0, 8)
            for kc in range(16):
                S.pe(lambda e, pb=pb, kc=kc: e.matmul(
                    bank(pb), lhsT=actT[:, kc, j * 128:j * 128 + 128], rhs=wo[n][:, kc, :],
                    start=(kc == 0), stop=(kc == 15)),
                    reads=actT_all + [wo_res[n]], writes=[("ps", pb)])
            S.dve(lambda e, pb=pb: e.tensor_tensor(
                out=x1[:, j, n * 512:n * 512 + 512], in0=bank(pb), in1=x1[:, j, n * 512:n * 512 + 512],
                op=ALU.add), reads=[("ps", pb), ("x1", j)], writes=[("x1", j)])

        def ln2_tile(j):
            hs = j % 4
            S.dve(lambda e: e.memset(ssq[:, 0, j:j + 1], 0.0), writes=[("ssq", 0, j)])
            S.act(lambda e: e.activation(out=junkB, in_=x1[:, j, :], func=AF.Square,
                                         accum_out=ssq[:, 0, j:j + 1]),
                  reads=[("x1", j)], writes=[("ssq", 0, j), "junk"])
            S.dve(lambda e: e.tensor_scalar(out=rstd[:, 0, j:j + 1], in0=ssq[:, 0, j:j + 1],
                                            scalar1=1.0 / D, scalar2=EPS, op0=ALU.mult, op1=ALU.add),
                  reads=[("ssq", 0, j)], writes=[("rstd2", j)])
            S.act(lambda e: e.activation(out=rstd[:, 0, j:j + 1], in_=rstd[:, 0, j:j + 1], func=AF.Ln),
                  reads=[("rstd2", j)], writes=[("rstd2", j)])
            S.act(lambda e: e.activation(out=rstd[:, 0, j:j + 1], in_=rstd[:, 0, j:j + 1], func=AF.Exp,
                                         scale=-0.5), reads=[("rstd2", j)], writes=[("rstd2", j)])
            S.dve(lambda e: e.scalar_tensor_tensor(
                out=hb2[hs], in0=x1[:, j, :], scalar=rstd[:, 0, j:j + 1], in1=g2,
                op0=ALU.mult, op1=ALU.mult),
                reads=[("x1", j), ("rstd2", j), "g2"], writes=[("hb2", hs)])

        def ln2_T(j):
            hs = j % 4
            for kh in range(2):
                tb = next_bank(0, 8)
                for k8 in range(8):
                    kc = 8 * kh + k8
                    S.pe(lambda e, tb=tb, k8=k8, kc=kc: e.transpose(
                        out=bankbf(tb)[:, k8 * 128:k8 * 128 + 128],
                        in_=hb2[hs][:, kc * 128:kc * 128 + 128], identity=ident[:]),
                        reads=[("hb2", hs), "ident"], writes=[("ps", tb)])
                evac_copy(actT[:, 8 * kh:8 * kh + 8, j * 128:j * 128 + 128],
                          bankbf(tb).rearrange("p (a b) -> p a b", b=128),
                          reads=[("ps", tb)], writes=[("h2T", j, kh)])

        for n in range(2):
            for j in range(8):
                oproj(j, n)
        for j in range(8):
            oproj(j, 2)
            oproj(j, 3)
            ln2_tile(j)
            if j >= 3:
                ln2_T(j - 3)
        for j in range(5, 8):
            ln2_T(j)
        h2T_all = [("h2T", j, kh) for j in range(8) for kh in range(2)]

        NG = DFF // 512

        def ffn_load(gi):
            ws = gi % 2
            S.dma("pool", lambda e, gi=gi, ws=ws: e.dma_start(
                out=wbA[ws], in_=w1_r[:, :, gi * 512:gi * 512 + 512]), writes=[("wbA", ws)],
                semkey=("dma", "pool", ("wbA", ws)))
            S.dma("pool", lambda e, gi=gi, ws=ws: e.dma_start(
                out=wbB[ws], in_=w2_r[:, gi, :, :]), writes=[("wbB", ws)])

        rl_ctr = [0]

        def ffn_A(gi):
            ws = gi % 2
            for fc in range(4):
                for half in range(2):
                    pb = next_bank(0, 8)
                    for kc in range(16):
                        S.pe(lambda e, pb=pb, kc=kc, fc=fc, half=half, ws=ws: e.matmul(
                            bank(pb), lhsT=wbA[ws][:, kc, fc * 128:fc * 128 + 128],
                            rhs=actT[:, kc, half * 512:half * 512 + 512],
                            start=(kc == 0), stop=(kc == 15)),
                            reads=h2T_all + [("wbA", ws)], writes=[("ps", pb)])
                    ri = rl_ctr[0] % 2
                    rl_ctr[0] += 1
                    S.act(lambda e, pb=pb, ri=ri: e.activation(out=rl[ri], in_=bank(pb), func=AF.Relu),
                          reads=[("ps", pb)], writes=[("rl", ri)])
                    ew_tt(dict(out=aT[ws][:, fc, half * 512:half * 512 + 512], in0=rl[ri], in1=rl[ri],
                               op=ALU.mult), [("rl", ri)], [("aT", ws, fc, half)])

        def final_tile(j):
            S.dve(lambda e: e.memset(ssq[:, 1, j:j + 1], 0.0), writes=[("ssq", 1, j)])
            S.act(lambda e: e.activation(out=junkB, in_=x1[:, j, :], func=AF.Square,
                                         accum_out=ssq[:, 1, j:j + 1]),
                  reads=[("x1", j)], writes=[("ssq", 1, j), "junk"])
            S.dve(lambda e: e.tensor_scalar(out=rstd[:, 1, j:j + 1], in0=ssq[:, 1, j:j + 1],
                                            scalar1=1.0 / D, scalar2=EPS, op0=ALU.mult, op1=ALU.add),
                  reads=[("ssq", 1, j)], writes=[("rstdf", j)])
            S.act(lambda e: e.activation(out=rstd[:, 1, j:j + 1], in_=rstd[:, 1, j:j + 1], func=AF.Ln),
                  reads=[("rstdf", j)], writes=[("rstdf", j)])
            S.act(lambda e: e.activation(out=rstd[:, 1, j:j + 1], in_=rstd[:, 1, j:j + 1], func=AF.Exp,
                                         scale=-0.5), reads=[("rstdf", j)], writes=[("rstdf", j)])
            S.dve(lambda e: e.scalar_tensor_tensor(
                out=x1[:, j, :], in0=x1[:, j, :], scalar=rstd[:, 1, j:j + 1], in1=gf,
                op0=ALU.mult, op1=ALU.mult),
                reads=[("x1", j), ("rstdf", j), "gf"], writes=[("x1", j)])
            S.dma("sp", lambda e: e.dma_start(out=y_r[:, O0 // 128 + j, :], in_=x1[:, j, :]),
                  reads=[("x1", j)], writes=[("y", u, j)], semkey=("y", j))

        def ffn_Y(gi):
            ws = gi % 2
            for j in range(8):
                if gi == NG - 1 and j >= 1:
                    final_tile(j - 1)
                for n in range(4):
                    pb = next_bank(0, 8)
                    for fc in range(4):
                        S.pe(lambda e, pb=pb, fc=fc, j=j, n=n, ws=ws: e.matmul(
                            bank(pb), lhsT=aT[ws][:, fc, j * 128:j * 128 + 128],
                            rhs=wbB[ws][:, fc, n * 512:n * 512 + 512],
                            start=(fc == 0), stop=(fc == 3)),
                            reads=[("aT", ws, fc, j // 4), ("wbB", ws)], writes=[("ps", pb)])
                    S.dve(lambda e, pb=pb, j=j, n=n: e.tensor_tensor(
                        out=x1[:, j, n * 512:n * 512 + 512], in0=bank(pb),
                        in1=x1[:, j, n * 512:n * 512 + 512], op=ALU.add),
                        reads=[("ps", pb), ("x1", j)], writes=[("x1", j)])

        S.dve(lambda e: e.memset(ssq[:, 1, 0:1], 0.0), reads=[("hb2", i_) for i_ in range(4)] + h2T_all,
              writes=[("aT", ws_, fc_, hf_) for ws_ in range(2) for fc_ in range(4) for hf_ in range(2)])
        ffn_load(0)
        ffn_A(0)
        for gi in range(NG):
            if gi + 1 < NG:
                ffn_load(gi + 1)
                ffn_A(gi + 1)
            ffn_Y(gi)

        final_tile(7)

    for u_, R0_ in enumerate((0, 1024, 4096)):
        do_unit(u_, R0_)

    S.emit()
    stack.close()
    return nc


def _alibi_slopes():
    return np.array([2.0 ** (-8.0 * (h + 1) / 16) for h in range(16)], dtype=np.float64)


def _wtab():
    sl = _alibi_slopes()
    i = np.arange(128)[:, None].astype(np.float64)
    w = np.zeros((128, 16, WT_COLS), dtype=np.float32)
    for h in range(16):
        col = 0
        for d in (1, 4):
            j = np.arange(128)[None, :].astype(np.float64)
            a = np.exp(-sl[h] * d * np.abs(j - i + 64)) * (j <= i)
            b = np.exp(-sl[h] * d * np.abs(j - i - 64)) * (j >= i)
            w[:, h, col:col + 128] = a
            w[:, h, col + 128:col + 256] = b
            col += 256
        j = np.arange(64)[None, :].astype(np.float64)
        a = np.exp(-sl[h] * 16 * np.abs(j - i + 64)) * (j <= i)
        b = np.exp(-sl[h] * 16 * np.abs(j - i - 64)) * (j >= i) * (i < 64)
        w[:, h, col:col + 64] = a
        w[:, h, col + 64:col + 128] = b
    return w.reshape(128, 16 * WT_COLS)


def _rcnt_row(pos, T):
    out = np.zeros((4, len(pos)), dtype=np.float32)
    for g, w in enumerate((2, 4, 8, 16)):
        lo = np.clip(pos - w // 2, 0, T)
        hi = np.clip(pos + w // 2, 0, T)
        out[g] = 1.0 / (hi - lo).astype(np.float32)
    return out


_NC_CACHE = {}


def kernel(x_prompt, x_sample, ln1_g, w_in, pool_w, pool_scale, w_out, ln2_g, w1, w2, final_g):
    x_prompt = np.asarray(x_prompt, dtype=np.float32)
    x_sample = np.asarray(x_sample, dtype=np.float32)
    f = lambda a: np.ascontiguousarray(np.asarray(a, dtype=np.float32))
    SEQ = x_prompt.shape[1]
    DSEQ = x_sample.shape[1]
    if "nc" not in _NC_CACHE:
        _NC_CACHE["nc"] = build_program()
    nc = _NC_CACHE["nc"]
    rep = lambda v: np.ascontiguousarray(np.broadcast_to(f(v).reshape(1, D), (128, D)))
    common = {
        "wtab": _wtab(),
        "g1": rep(ln1_g), "g2": rep(ln2_g), "gf": rep(final_g),
        "w_in": f(w_in).reshape(D, 4096), "w_out": f(w_out).reshape(D, D),
        "w1": f(w1).reshape(D, DFF), "w2": f(w2).reshape(DFF, D),
        "pool_w": f(pool_w).reshape(4, 256, 256),
        "pscale": np.ascontiguousarray(f(pool_scale).reshape(8, 128).T),
        "ident": np.eye(128, dtype=np.float32),
    }
    in_maps = []
    for c in range(NCORES):
        xr = np.zeros((NREG, D), dtype=np.float32)
        valid = np.zeros((NREG,), dtype=np.float32)
        p0 = 2048 * c - 1024
        lo, hi = max(p0, 0), min(p0 + 4096, SEQ)
        xr[lo - p0:hi - p0] = x_prompt[0, lo:hi]
        valid[lo - p0:hi - p0] = 1.0
        s, hf = c // 2, c % 2
        seq = x_sample[s][::-1] if hf else x_sample[s]
        xr[4096 + 1024:4096 + 3072] = seq
        valid[4096 + 1024:4096 + 3072] = 1.0
        vmask = np.ascontiguousarray(valid.reshape(56, 128).T)
        pos_p = np.arange(2048 * c, 2048 * c + 2048)
        pos_s = (DSEQ - 1 - np.arange(1024)) if hf else np.arange(1024)
        rc = np.concatenate([_rcnt_row(pos_p, SEQ), _rcnt_row(pos_s, DSEQ)], axis=1)
        rcnt = np.ascontiguousarray(np.broadcast_to(rc.reshape(1, 4 * NOWN), (128, 4 * NOWN)))
        m = dict(common)
        rcB = np.zeros((4, 1024), dtype=np.float32)
        if hf:
            rcB[:] = rc[:, 2048:3072]
            rc[:, 2048:3072] = 0.0
        rcnt = np.ascontiguousarray(np.broadcast_to(rc.reshape(1, 4 * NOWN), (128, 4 * NOWN)))
        rcntB = np.ascontiguousarray(np.broadcast_to(rcB.reshape(1, 4 * 1024), (128, 4 * 1024)))
        m.update({"xr": xr, "vmask": vmask, "rcnt": rcnt, "rcntB": rcntB})
        in_maps.append(m)
    res = run_bass_kernel_spmd(nc, in_maps, core_ids=list(range(NCORES)))
    y_prompt = np.zeros_like(x_prompt)
    y_sample = np.zeros_like(x_sample)
    for c in range(NCORES):
        yc = np.asarray(res.results[c]["y"], dtype=np.float32)
        y_prompt[0, 2048 * c:2048 * c + 2048] = yc[0:2048]
        s, hf = c // 2, c % 2
        if hf:
            y_sample[s, 1024:2048] = yc[2048:3072][::-1]
        else:
            y_sample[s, 0:1024] = yc[2048:3072]
    return (y_prompt, y_sample)
```

```python
import contextlib
import numpy as np
import concourse.bass as bass
import concourse.mybir as mybir
from concourse.bass_utils import run_bass_kernel_spmd

F32 = mybir.dt.float32
BF16 = mybir.dt.bfloat16
U8 = mybir.dt.uint8
AF = mybir.ActivationFunctionType
ALU = mybir.AluOpType

D = 2048
DFF = 8192
NREG = 7168
NOWN = 3072
EPS = 1e-6
NCORES = 8
ENGS = ("pe", "act", "dve", "pool", "sp")
EPOCH = 16000
WT_COLS = 640


class Op:
    __slots__ = ("eng", "fn", "reads", "writes", "dma", "deps", "marked", "sem", "val",
                 "idx", "semkey")

    def __init__(self, eng, fn, reads, writes, dma, semkey):
        self.eng, self.fn, self.reads, self.writes, self.dma = eng, fn, reads, writes, dma
        self.deps = []
        self.marked = False
        self.sem = None
        self.val = 0
        self.semkey = semkey


class Sched:
    def __init__(self, nc):
        self.nc = nc
        self.ops = []
        self.last_w = {}
        self.readers = {}
        self.last_eng = {}
        self.last_dma = {}
        self.pending_barrier = {}

    def barrier(self):
        snap = list(self.last_eng.values()) + list(self.last_dma.values())
        for e in ENGS:
            self.pending_barrier[e] = snap

    def op(self, eng, fn, reads=(), writes=(), dma=False, semkey=None):
        o = Op(eng, fn, tuple(reads), tuple(writes), dma, semkey)
        o.idx = len(self.ops)
        deps = set()
        for r in o.reads:
            w = self.last_w.get(r)
            if w is not None:
                deps.add(w)
        for w_ in o.writes:
            w = self.last_w.get(w_)
            if w is not None:
                deps.add(w)
            for rd in self.readers.get(w_, ()):
                deps.add(rd)
        for r in o.reads:
            self.readers.setdefault(r, []).append(o)
        for w_ in o.writes:
            self.last_w[w_] = o
            self.readers[w_] = []
        pb = self.pending_barrier.pop(eng, None)
        if pb:
            for d in pb:
                if d.eng == eng and not d.dma:
                    continue
                deps.add(d)
        deps.discard(o)
        for d in deps:
            if d.eng == "pe" and o.eng == "pe" and not d.dma and not o.dma:
                continue
            o.deps.append(d)
            d.marked = True
        self.ops.append(o)
        if dma:
            self.last_dma[semkey] = o
        else:
            self.last_eng[eng] = o
        return o

    def pe(self, fn, reads=(), writes=()):
        return self.op("pe", fn, reads, writes)

    def act(self, fn, reads=(), writes=()):
        return self.op("act", fn, reads, writes)

    def dve(self, fn, reads=(), writes=()):
        return self.op("dve", fn, reads, writes)

    def pool(self, fn, reads=(), writes=()):
        return self.op("pool", fn, reads, writes)

    def dma(self, eng, fn, reads=(), writes=(), semkey=None):
        if semkey is None:
            semkey = ("dma", eng, writes[0] if writes else reads[0])
        return self.op(eng, fn, reads, writes, dma=True, semkey=semkey)

    def emit(self):
        nc = self.nc
        stack = contextlib.ExitStack()
        sem_cache = {}

        def get_sem(key):
            if key not in sem_cache:
                sem_cache[key] = stack.enter_context(nc.semaphore("s%d" % len(sem_cache)))
            return sem_cache[key]

        eng_cnt = {e: 0 for e in ENGS}
        eng_epoch = {e: 0 for e in ENGS}
        dma_cnt = {}
        last_dma_on_key = {}
        for o in self.ops:
            if o.dma:
                k = o.semkey
                dma_cnt[k] = dma_cnt.get(k, 0) + 16
                o.sem = get_sem(k)
                o.val = dma_cnt[k]
                o.marked = True
                last_dma_on_key[k] = o
            elif o.marked:
                e = o.eng
                if eng_cnt[e] >= EPOCH:
                    eng_epoch[e] += 1
                    eng_cnt[e] = 0
                eng_cnt[e] += 1
                o.sem = get_sem(("eng", e, eng_epoch[e]))
                o.val = eng_cnt[e]
        self.n_sems = len(sem_cache)

        per_eng = {e: [] for e in ENGS}
        for o in self.ops:
            per_eng[o.eng].append(o)

        dma_issued = {}
        for o in self.ops:
            if o.dma:
                dma_issued.setdefault(o.semkey, []).append((o.idx, o.val))

        def dma_wait_val(dep, consumer_idx):
            return dep.val

        final_waits = [(o.sem, dma_cnt[k]) for k, o in last_dma_on_key.items()]

        def run_engine(ename, eng):
            waited = {}
            for o in per_eng[ename]:
                need = {}
                for d in o.deps:
                    v = dma_wait_val(d, o.idx) if d.dma else d.val
                    key = id(d.sem)
                    if key not in need or need[key][1] < v:
                        need[key] = (d.sem, v)
                for key, (sem, v) in need.items():
                    if waited.get(key, 0) >= v:
                        continue
                    eng.wait_ge(sem, v)
                    waited[key] = v
                ins = o.fn(eng)
                if o.marked:
                    ins.then_inc(o.sem, 16 if o.dma else 1)
            if ename == "sp":
                for (sem, v) in final_waits:
                    eng.wait_ge(sem, v)

        with nc.Block() as block:
            @block.tensor
            def _(e):
                run_engine("pe", e)

            @block.scalar
            def _(e):
                run_engine("act", e)

            @block.vector
            def _(e):
                run_engine("dve", e)

            @block.gpsimd
            def _(e):
                run_engine("pool", e)

            @block.sync
            def _(e):
                run_engine("sp", e)
        stack.close()


_DTSIZE = {F32: 4, BF16: 2}


class Arena:
    def __init__(self, t):
        self.t = t

    def view(self, off, shape, dt):
        n = 1
        for s in shape[1:]:
            n *= s
        nbytes = n * _DTSIZE[dt]
        assert off % 4 == 0 and off + nbytes <= self.t.shape[1], (off, nbytes, self.t.shape)
        ap = self.t[:, off:off + nbytes].bitcast(dt)
        if len(shape) == 3:
            ap = ap.rearrange("p (a b) -> p a b", b=shape[2])
        elif len(shape) == 4:
            ap = ap.rearrange("p (a b c) -> p a b c", b=shape[2], c=shape[3])
        return ap


def bcast_mid(ap2d, reps):
    a = ap2d.ap
    return bass.AP(ap2d.tensor, ap2d.offset, [list(a[0]), [0, reps], list(a[1])])


KB = 1024


def ss(start, n, step):
    return slice(start, start + (n - 1) * step + 1, step)


def build_program():
    nc = bass.Bass("TRN2", target_bir_lowering=False)

    def din(name, shape, dt=F32):
        return nc.dram_tensor(name, list(shape), dt, kind="ExternalInput").ap()

    xr = din("xr", [NREG, D])
    vmask_d = din("vmask", [128, 56])
    rcnt_d = din("rcnt", [128, 4 * NOWN])
    wtab_d = din("wtab", [128, 16 * WT_COLS])
    g1_d = din("g1", [128, D])
    g2_d = din("g2", [128, D])
    gf_d = din("gf", [128, D])
    w_in = din("w_in", [D, 4096])
    w_out = din("w_out", [D, D])
    w1 = din("w1", [D, DFF])
    w2 = din("w2", [DFF, D])
    pool_w = din("pool_w", [4, 256, 256])
    pscale_d = din("pscale", [128, 8])
    ident_d = din("ident", [128, 128])
    rcntB_d = din("rcntB", [128, 4 * 1024])
    y = nc.dram_tensor("y", [NOWN, D], F32, kind="ExternalOutput").ap()
    KT = nc.dram_tensor("KT", [1024, NREG], BF16).ap()
    VA = nc.dram_tensor("VA", [NREG, 2048], BF16).ap()
    QT = nc.dram_tensor("QT", [1024, NOWN], BF16).ap()
    UT = nc.dram_tensor("UT", [1024, NREG], F32).ap()
    WTB = nc.dram_tensor("WTB", [128, 16 * WT_COLS], BF16).ap()
    PWB = nc.dram_tensor("PWB", [4, 256, 256], BF16).ap()
    WOB = nc.dram_tensor("WOB", [D, D], BF16).ap()

    S = Sched(nc)
    stack = contextlib.ExitStack()
    ARENA_BYTES = 200 * KB
    arena_t = stack.enter_context(nc.sbuf_tensor("arena", [128, ARENA_BYTES], U8))
    A = Arena(arena_t)
    ident = stack.enter_context(nc.sbuf_tensor("ident_sb", [128, 128], BF16))
    ones3 = stack.enter_context(nc.sbuf_tensor("ones3", [128, 16, 64], BF16))
    vm = stack.enter_context(nc.sbuf_tensor("vm_sb", [128, 56], F32))
    pscale = stack.enter_context(nc.sbuf_tensor("pscale_sb", [128, 8], F32))
    ssq = stack.enter_context(nc.sbuf_tensor("ssq", [128, 2, 8], F32))
    rstd = stack.enter_context(nc.sbuf_tensor("rstd", [128, 2, 8], F32))
    ps = stack.enter_context(nc.psum_tensor("ps", [128, 4096], F32))

    def bank(b):
        return ps[:, 512 * b:512 * b + 512]

    def bankbf(b):
        return ps[:, 512 * b:512 * b + 512].bitcast(BF16)

    bank_rr = [0]

    def next_bank(lo=0, hi=8):
        b = lo + bank_rr[0] % (hi - lo)
        bank_rr[0] += 1
        return b

    evac_rr = [0]

    def evac_copy(out, in_, reads, writes, scale=None):
        evac_rr[0] += 1
        if evac_rr[0] % 2 == 0:
            if scale is None:
                S.act(lambda e: e.activation(out=out, in_=in_, func=AF.Copy), reads, writes)
            else:
                S.act(lambda e: e.activation(out=out, in_=in_, func=AF.Copy, scale=scale), reads, writes)
        else:
            if scale is None:
                S.dve(lambda e: e.tensor_copy(out=out, in_=in_), reads, writes)
            else:
                S.dve(lambda e: e.tensor_scalar(out=out, in0=in_, scalar1=scale, scalar2=None,
                                                op0=ALU.mult), reads, writes)

    S.dma("pool", lambda e: e.dma_start(out=ident[:], in_=ident_d[:]), writes=["ident"])
    S.dma("sp", lambda e: e.dma_start(out=vm[:], in_=vmask_d[:]), writes=["vm"])
    S.dma("sp", lambda e: e.dma_start(out=pscale[:], in_=pscale_d[:]), writes=["pscale"])
    S.pool(lambda e: e.memset(ones3[:], 1.0), writes=["ones3"])
    S.dma("pool", lambda e: e.dma_start(out=WTB[:], in_=wtab_d[:]), writes=["WTB"])
    S.dma("pool", lambda e: e.dma_start(out=PWB[:], in_=pool_w[:]), writes=["PWB"])

    def rms_stats(src_tiles, slot, n, reads, junk):
        S.dve(lambda e: e.memset(ssq[:, slot, 0:n], 0.0), writes=[("ssq", slot, j) for j in range(n)])
        for j in range(n):
            S.act(lambda e, j=j: e.activation(out=junk, in_=src_tiles(j), func=AF.Square,
                                              accum_out=ssq[:, slot, j:j + 1]),
                  reads=reads(j), writes=[("ssq", slot, j), "junk"])
        S.dve(lambda e: e.tensor_scalar(out=rstd[:, slot, 0:n], in0=ssq[:, slot, 0:n],
                                        scalar1=1.0 / D, scalar2=EPS, op0=ALU.mult, op1=ALU.add),
              reads=[("ssq", slot, j) for j in range(n)], writes=[("rstd", slot)])
        S.act(lambda e: e.activation(out=rstd[:, slot, 0:n], in_=rstd[:, slot, 0:n], func=AF.Ln),
              reads=[("rstd", slot)], writes=[("rstd", slot)])
        S.act(lambda e: e.activation(out=rstd[:, slot, 0:n], in_=rstd[:, slot, 0:n], func=AF.Exp, scale=-0.5),
              reads=[("rstd", slot)], writes=[("rstd", slot)])

    wA = A.view(0, [128, 16, 2048], BF16)
    xt = A.view(64 * KB, [128, 4, 2048], F32)
    hb = A.view(96 * KB, [128, 4, 2048], BF16)
    hT = [A.view(112 * KB + 16 * KB * i, [128, 16, 512], BF16) for i in range(2)]
    st1 = A.view(144 * KB, [128, 8, 512], BF16)
    st2_va = A.view(152 * KB, [128, 4, 16, 128], BF16)
    st2_va_flat = A.view(152 * KB, [128, 4, 2048], BF16)
    st2_u = A.view(180 * KB, [128, 8, 512], F32)
    g1 = A.view(168 * KB, [128, 2048], F32)
    junkA = A.view(176 * KB, [128, 2048], BF16)

    S.dma("sp", lambda e: e.dma_start(out=g1, in_=g1_d[:]), writes=["g1"])
    w_in_r = w_in.rearrange("(kc p) n -> p kc n", p=128)
    KT_r = KT.rearrange("(m p) t -> p m t", p=128)
    QT_r = QT.rearrange("(m p) t -> p m t", p=128)
    UT_r = UT.rearrange("(m p) t -> p m t", p=128)
    VA_r = VA.rearrange("(j p) c -> p j c", p=128)
    xr_r = xr.rearrange("(j p) f -> p j f", p=128)

    def load_wA(colsets):
        for (dc, sc, n) in colsets:
            for cg in range(n // 512):
                S.dma("pool", lambda e, dc=dc, sc=sc, cg=cg: e.dma_start(
                    out=wA[:, :, dc + 512 * cg:dc + 512 * cg + 512],
                    in_=w_in_r[:, :, sc + 512 * cg:sc + 512 * cg + 512]),
                    writes=[("wA", dc // 1024, cg)], semkey=("wA", dc // 1024, cg))

    def a_load(row0, nt):
        S.dma("sp", lambda e: e.dma_start(out=xt[:, 0:nt, :], in_=xr_r[:, row0 // 128:row0 // 128 + nt, :]),
              writes=["xt"])

    def a_sq(k, nt):
        rms_stats(lambda j: xt[:, j, :], k % 2, nt, lambda j: ["xt"], junkA)

    def a_stt(k, nt):
        slot = k % 2
        for j in range(nt):
            S.dve(lambda e, j=j: e.scalar_tensor_tensor(out=hb[:, j, :], in0=xt[:, j, :],
                                                        scalar=rstd[:, slot, j:j + 1], in1=g1,
                                                        op0=ALU.mult, op1=ALU.mult),
                  reads=["xt", ("rstd", slot), "g1"], writes=[("hb", j)])

    def a_T(k, nt):
        slot = k % 2
        for kp in range(8):
            tb = next_bank(0, 3)
            for kk in range(2):
                for j in range(nt):
                    kc = 2 * kp + kk
                    S.pe(lambda e, tb=tb, kk=kk, j=j, kc=kc: e.transpose(
                        out=bankbf(tb)[:, kk * 512 + j * 128:kk * 512 + j * 128 + 128],
                        in_=hb[:, j, kc * 128:kc * 128 + 128], identity=ident[:]),
                        reads=[("hb", j), "ident"], writes=[("ps", tb)])
            dst = hT[slot][:, 2 * kp:2 * kp + 2, 0:128 * nt]
            evac_copy(dst, bankbf(tb).rearrange("p (a b) -> p a b", b=512)[:, :, 0:128 * nt],
                      reads=[("ps", tb)], writes=[("hT", slot, kp)])

    def run_items(items):
        n = len(items)
        ld = lambda i: a_load(items[i][0], items[i][1])
        sq = lambda i: a_sq(i, items[i][1])
        st = lambda i: a_stt(i, items[i][1])
        tr = lambda i: a_T(i, items[i][1])
        ld(0); sq(0); st(0)
        if n > 1:
            ld(1)
        tr(0)
        if n > 1:
            sq(1); st(1)
        if n > 2:
            ld(2); sq(2)
        for i in range(n):
            row0, nt, fa, fb, pre = items[i]
            if pre is not None:
                pre()
            fa(row0, i % 2)
            if i + 1 < n:
                tr(i + 1)
            if i + 2 < n:
                st(i + 2)
            if i + 3 < n:
                ld(i + 3)
            fb(row0, i % 2)
            if i + 3 < n:
                sq(i + 3)

    def hT_reads(slot):
        return [("hT", slot, kp) for kp in range(8)]

    def proj_fm(slot, wcol0, stage, scale, dst_r, dst_t0, name, ntok=512):
        for m in range(8):
            pb = next_bank(3, 8)
            for kc in range(16):
                S.pe(lambda e, pb=pb, kc=kc, m=m: e.matmul(
                    bank(pb)[:, 0:ntok], lhsT=wA[:, kc, wcol0 + m * 128:wcol0 + m * 128 + 128],
                    rhs=hT[slot][:, kc, 0:ntok], start=(kc == 0), stop=(kc == 15)),
                    reads=hT_reads(slot) + [("wA", wcol0 // 1024, m // 4)], writes=[("ps", pb)])
            evac_copy(stage[:, m, 0:ntok], bank(pb)[:, 0:ntok], reads=[("ps", pb)], writes=[(name, m)],
                      scale=scale)
        S.dma("sp", lambda e: e.dma_start(out=dst_r[:, :, dst_t0:dst_t0 + ntok], in_=stage[:, :, 0:ntok]),
              reads=[(name, m) for m in range(8)], writes=[(name + "_d", dst_t0)], semkey=("out", name))

    def proj_v(row0, slot):
        b = row0 // 512
        for j in range(4):
            for n in range(2):
                pb = next_bank(3, 8)
                for kc in range(16):
                    S.pe(lambda e, pb=pb, kc=kc, j=j, n=n: e.matmul(
                        bank(pb), lhsT=hT[slot][:, kc, j * 128:j * 128 + 128],
                        rhs=wA[:, kc, 1024 + n * 512:1024 + n * 512 + 512],
                        start=(kc == 0), stop=(kc == 15)),
                        reads=hT_reads(slot) + [("wA", 1, n)], writes=[("ps", pb)])
                evac_copy(st2_va[:, j, 8 * n:8 * n + 8, 0:64],
                          bank(pb).rearrange("p (a b) -> p a b", b=64),
                          reads=[("ps", pb)], writes=[("st2", j, n)])
            S.act(lambda e, j=j: e.activation(out=st2_va[:, j, :, 64:128], in_=ones3[:], func=AF.Copy,
                                              scale=vm[:, 4 * b + j:4 * b + j + 1]),
                  reads=["ones3", "vm"], writes=[("st2v", j)])
        S.dma("sp", lambda e: e.dma_start(out=VA_r[:, 4 * b:4 * b + 4, :], in_=st2_va_flat),
              reads=[("st2", j, n) for j in range(4) for n in range(2)] + [("st2v", j) for j in range(4)],
              writes=[("VA_d", b)], semkey=("out", "VA"))

    own_off = {2: 0, 3: 512, 4: 1024, 5: 1536, 10: 2048, 11: 2560}
    nop = lambda row0, sl: None
    fa1 = lambda row0, sl: proj_fm(sl, 0, st1, None, KT_r, row0, "st1")
    fb1 = lambda row0, sl: proj_v(row0, sl)
    fa2 = lambda row0, sl: proj_fm(sl, 0, st2_u, None, UT_r, row0, "st2u")
    fa2m = lambda row0, sl: proj_fm(sl, 0, st2_u, None, UT_r, row0, "st2u", ntok=128)
    fb2 = lambda row0, sl: proj_fm(sl, 1024, st1, 0.125, QT_r, own_off[row0 // 512], "st1")
    items = [(512 * b, 4, fa1, fb1, None) for b in range(14) if b not in (8, 9)]
    S.dve(lambda e: e.memset(hT[1].rearrange("p a b -> p (a b)"), 0.0), writes=[("hT", 1, kp) for kp in range(8)])
    zsrc = hT[1].rearrange("p a b -> p (a b)")
    S.dma("sp", lambda e: e.dma_start(out=KT_r[:, :, 4096:5120],
                                      in_=zsrc.rearrange("p (m t) -> p m t", t=1024)),
          reads=[("hT", 1, kp) for kp in range(8)], writes=[("zKT", 0)])
    for hh in range(2):
        S.dma("sp", lambda e, hh=hh: e.dma_start(out=VA_r[:, 32 + 4 * hh:32 + 4 * hh + 4, :],
                                                 in_=zsrc.rearrange("p (j c) -> p j c", c=2048)),
              reads=[("hT", 1, kp) for kp in range(8)], writes=[("zVA", hh)])
    pass2 = [(1024, 4, fa2, fb2), (896, 1, fa2m, nop), (1536, 4, fa2, fb2), (3072, 1, fa2m, nop),
             (2048, 4, fa2, fb2), (4992, 1, fa2m, nop), (2560, 4, fa2, fb2), (6144, 1, fa2m, nop),
             (5120, 4, fa2, fb2), (5632, 4, fa2, fb2)]
    for i_, (row0, nt, fa, fb) in enumerate(pass2):
        pre = (lambda: load_wA([(0, 3072, 1024), (1024, 0, 1024)])) if i_ == 0 else None
        items.append((row0, nt, fa, fb, pre))
    for q_, ii in enumerate((3, 5, 7, 9)):
        it = items[ii]
        items[ii] = (it[0], it[1], it[2], it[3],
                     lambda q_=q_: S.dma("pool", lambda e: e.dma_start(
                         out=WOB[512 * q_:512 * q_ + 512, :], in_=w_out[512 * q_:512 * q_ + 512, :]),
                         reads=[("st1", 7)], writes=[("WOB", q_)], semkey=("WOB", q_)))
    load_wA([(0, 1024, 1024), (1024, 2048, 1024)])
    run_items(items)

    actT = A.view(0, [128, 16, 1024], BF16)
    g2 = A.view(32 * KB, [128, 2048], F32)
    gf = A.view(40 * KB, [128, 2048], F32)
    junkB = A.view(48 * KB, [128, 2048], BF16)
    BA = 52 * KB
    o = BA
    qt = [A.view(o + 2 * KB * i, [128, 1024], BF16) for i in range(2)]; o += 4 * KB
    kt = [A.view(o + 6 * KB * i, [128, 3072], BF16) for i in range(2)]; o += 12 * KB
    va1 = A.view(o, [128, 9, 256], BF16); o += 9 * 512
    va4 = A.view(o, [128, 4, 3, 256], BF16); o += 6 * KB
    va16a = A.view(o, [128, 16, 256], BF16); o += 8 * KB
    va16b = A.view(o, [128, 16, 256], BF16); o += 8 * KB
    wt = A.view(o, [128, 16, WT_COLS], BF16); o += 20 * KB
    NSL = 8
    et = [A.view(o + KB * i, [128, 512], BF16) for i in range(NSL)]; o += 8 * KB
    pt = [A.view(o + KB * i, [128, 512], BF16) for i in range(NSL)]; o += 8 * KB
    rc = [A.view(o + 4 * KB * i, [128, 1024], F32) for i in range(2)]; o += 8 * KB
    kt16 = A.view(o, [128, 16, 192], BF16); o += 6 * KB
    kt4 = A.view(o, [128, 4, 768], BF16); o += 6 * KB
    qt16 = A.view(o, [128, 16, 64], BF16); o += 2 * KB
    qt4 = A.view(o, [128, 4, 256], BF16); o += 2 * KB
    UTW = 1040
    ut = [A.view(o + 4160 * i, [128, UTW], F32) for i in range(2)]; o += 2 * 4160
    sa = A.view(o, [128, UTW], F32); o += 4160
    sb = A.view(o, [128, UTW], F32); o += 4160
    rcg = [A.view(o + 4 * KB * i, [128, 1024], F32) for i in range(2)]; o += 8 * KB
    pooledT = A.view(o, [128, 8, 1024], BF16); o += 16 * KB
    pw = A.view(o, [128, 4, 2, 256], BF16); o += 4 * KB
    assert o <= ARENA_BYTES, o
    x1 = A.view(BA, [128, 8, 2048], F32)
    wbA = [A.view(BA + 64 * KB + 16 * KB * i, [128, 16, 512], BF16) for i in range(2)]
    wbB = [A.view(BA + 96 * KB + 16 * KB * i, [128, 4, 2048], BF16) for i in range(2)]
    wbB_o = [A.view(BA + 96 * KB + 16 * KB * i, [128, 16, 512], BF16) for i in range(2)]
    hb2 = [A.view(BA + 128 * KB + 4 * KB * i, [128, 2048], BF16) for i in range(4)]
    aT = [A.view(BA + 128 * KB + 8 * KB * i, [128, 4, 1024], BF16) for i in range(2)]
    rl = [A.view(BA + 144 * KB + 2 * KB * i, [128, 512], F32) for i in range(2)]
    assert BA + 148 * KB <= ARENA_BYTES

    WOB_r = WOB.rearrange("(kc p) n -> p kc n", p=128)
    w1_r = w1.rearrange("(kc p) n -> p kc n", p=128)
    w2_r = w2.rearrange("(g fc p) n -> p g fc n", p=128, fc=4)
    pool_w_r = PWB.rearrange("g (cc p) d -> p g cc d", p=128)
    y_r = y.rearrange("(j p) f -> p j f", p=128)
    wb_use = [0]
    has_pool_tt = hasattr(nc.gpsimd, "tensor_tensor")

    def ew_tt(fn_args, reads, writes):
        if has_pool_tt:
            S.pool(lambda e: e.tensor_tensor(**fn_args), reads, writes)
        else:
            S.dve(lambda e: e.tensor_tensor(**fn_args), reads, writes)

    def ew_dve(fn_args, reads, writes):
        S.dve(lambda e: e.tensor_tensor(**fn_args), reads, writes)

    def do_unit(u, R0):
        O0 = 1024 * u
        S.barrier()
        if u == 0:
            S.dma("sp", lambda e: e.dma_start(out=g2, in_=g2_d[:]), writes=["g2"])
            S.dma("sp", lambda e: e.dma_start(out=gf, in_=gf_d[:]), writes=["gf"])
        S.dma("sp", lambda e: e.dma_start(out=wt.rearrange("p a b -> p (a b)"), in_=WTB[:]),
              reads=["WTB"], writes=["wt"])

        batches = []
        head_ctr = 0
        def load_qk(hp):
            sl = hp % 2
            S.dma("sp", lambda e: e.dma_start(
                out=qt[sl], in_=QT[hp * 128:hp * 128 + 128, O0:O0 + 1024]), writes=[("qt", sl)])
            S.dma("sp", lambda e: e.dma_start(
                out=kt[sl], in_=KT[hp * 128:hp * 128 + 128, R0:R0 + 3072]), writes=[("kt", sl)])

        def va_src(hp, row0, pat):
            return bass.AP(VA.tensor, VA.offset + row0 * 2048 + hp * 256, pat)

        def load_va1(hp):
            S.dma("sp", lambda e, s=va_src(hp, R0 + 960, [[2048, 128], [128 * 2048, 9], [1, 256]]): e.dma_start(
                out=va1, in_=s), writes=["va1"])

        def load_va4(hp):
            for c in range(4):
                S.dma("sp", lambda e, c=c, s=va_src(hp, R0 + 768 + c, [[4 * 2048, 128], [512 * 2048, 3], [1, 256]]):
                      e.dma_start(out=va4[:, c, :, :], in_=s), writes=[("va4", c)])

        def load_va16(hp):
            S.dma("sp", lambda e, s=va_src(hp, R0, [[16 * 2048, 128], [2048, 16], [1, 256]]): e.dma_start(
                out=va16a, in_=s), writes=["va16a"])
            S.dma("sp", lambda e, s=va_src(hp, R0 + 2048, [[16 * 2048, 64], [2048, 16], [1, 256]]): e.dma_start(
                out=va16b[0:64, :, :], in_=s), writes=["va16b"])

        def copy4(hp):
            sl = hp % 2
            S.dve(lambda e: e.tensor_copy(out=kt4, in_=kt[sl].rearrange("p (i c) -> p c i", c=4)),
                  reads=[("kt", sl)], writes=["kt4"])
            S.act(lambda e: e.activation(out=qt4, in_=qt[sl].rearrange("p (i c) -> p c i", c=4),
                                         func=AF.Copy), reads=[("qt", sl)], writes=["qt4"])

        def copy16(hp):
            sl = hp % 2
            S.dve(lambda e: e.tensor_copy(out=kt16, in_=kt[sl].rearrange("p (i c) -> p c i", c=16)),
                  reads=[("kt", sl)], writes=["kt16"])
            S.act(lambda e: e.activation(out=qt16, in_=qt[sl].rearrange("p (i c) -> p c i", c=16),
                                         func=AF.Copy), reads=[("qt", sl)], writes=["qt16"])

        pre_qk = {}
        post_pv = {}
        post_qk = {}
        for hp in range(8):
            if hp + 1 < 8:
                pre_qk.setdefault(24 * hp, []).append(lambda hp=hp: load_qk(hp + 1))
                post_pv.setdefault(24 * hp + 15, []).append(lambda hp=hp: load_va1(hp + 1))
                post_pv.setdefault(24 * hp + 19, []).append(lambda hp=hp: load_va4(hp + 1))
                post_pv.setdefault(24 * hp + 23, []).append(lambda hp=hp: load_va16(hp + 1))
                post_qk.setdefault(24 * hp + 19, []).append(lambda hp=hp: copy4(hp + 1))
                post_qk.setdefault(24 * hp + 23, []).append(lambda hp=hp: copy16(hp + 1))
        load_qk(0)
        copy4(0)
        copy16(0)
        load_va1(0)
        load_va4(0)
        load_va16(0)
        S.dma("sp", lambda e: e.dma_start(out=pw, in_=pool_w_r), reads=["PWB"], writes=["pw"])
        def pool_tile(m):
            g = m // 2
            us = m % 2
            S.dma("pool", lambda e, m=m, us=us: e.dma_start(
                out=ut[us], in_=UT[m * 128:m * 128 + 128, R0 + 1024 - 8:R0 + 1024 - 8 + UTW]),
                writes=[("ut", us)])
            if m % 2 == 0:
                if u == 2:
                    S.dma("pool", lambda e, g=g: e.dma_start(
                        out=rcg[0], in_=rcnt_d[:, g * NOWN + O0:g * NOWN + O0 + 1024]), writes=[("rcg", 0)])
                    S.dma("pool", lambda e, g=g: e.dma_start(
                        out=rcg[1], in_=rcntB_d[:, g * 1024:g * 1024 + 1024]), writes=[("rcg", 1)])
                else:
                    gs = g % 2
                    S.dma("pool", lambda e, g=g, gs=gs: e.dma_start(
                        out=rcg[gs], in_=rcnt_d[:, g * NOWN + O0:g * NOWN + O0 + 1024]), writes=[("rcg", gs)])
            uu = ut[us]
            ew_dve(dict(out=sa[:, 1:1040], in0=uu[:, 0:1039], in1=uu[:, 1:1040], op=ALU.add),
                  [("ut", us)], ["sa"])
            cur, oth, curname, othname = sa, sb, "sa", "sb"
            lo, hi = 1, 1040
            sh = 1
            for lvl in range(g):
                nlo, nhi = lo + sh, hi - sh
                ew_dve(dict(out=oth[:, nlo:nhi], in0=cur[:, nlo - sh:nhi - sh], in1=cur[:, nlo + sh:nhi + sh],
                           op=ALU.add), [curname], [othname])
                cur, oth, curname, othname = oth, cur, othname, curname
                lo, hi = nlo, nhi
                sh *= 2
            if u == 2:
                ew_dve(dict(out=oth[:, 8:1032], in0=cur[:, 8:1032], in1=rcg[0], op=ALU.mult),
                      [curname, ("rcg", 0)], [othname])
                ew_dve(dict(out=oth[:, 8:1032], in0=oth[:, 8:1032], in1=uu[:, 8:1032], op=ALU.subtract),
                      [othname, ("ut", us)], [othname])
                ew_dve(dict(out=cur[:, 9:1033], in0=cur[:, 9:1033], in1=rcg[1], op=ALU.mult),
                      [curname, ("rcg", 1)], [curname])
                ew_dve(dict(out=pooledT[:, m, :], in0=oth[:, 8:1032], in1=cur[:, 9:1033], op=ALU.add),
                      [othname, curname], [("pooledT", m)])
            else:
                gs = g % 2
                ew_dve(dict(out=oth[:, 8:1032], in0=cur[:, 8:1032], in1=rcg[gs], op=ALU.mult),
                      [curname, ("rcg", gs)], [othname])
                ew_dve(dict(out=pooledT[:, m, :], in0=oth[:, 8:1032], in1=uu[:, 8:1032], op=ALU.subtract),
                      [othname, ("ut", us)], [("pooledT", m)])
        for m_ in range(8):
            post_qk.setdefault(4 + 20 * m_, []).append(lambda m_=m_: pool_tile(m_))
        for hp in range(8):
            sl = hp % 2
            for hl in range(2):
                h = 2 * hp + hl
                hbp = 64 * hl
                aset = head_ctr % 2
                head_ctr += 1
                accb = 1024 * aset
                first = {0: True, 1: True}
                vcol = hl * 128
                qv = qt[sl]
                kv = kt[sl]
                hb_list = []
                for bb in range(4):
                    tiles = []
                    for bi in range(2):
                        b = 2 * bb + bi
                        q_ap = qv[hbp:hbp + 64, 128 * b:128 * b + 128]
                        for ti in range(2):
                            k = b + ti
                            k_ap = kv[hbp:hbp + 64, 960 + 128 * k:960 + 128 * k + 128]
                            v_ap = va1[:, k, vcol:vcol + 128]
                            outs = [(b // 4, ps[:, accb + 128 * b:accb + 128 * b + 128], 0, 128)]
                            tiles.append(dict(slot=(2 * bi + ti) * 128, nq=128, nk=128, k=k_ap, q=q_ap,
                                              v=v_ap, outs=outs, vres="va1", qkres=[("kt", sl), ("qt", sl)]))
                    tiles = [tiles[0], tiles[2], tiles[1], tiles[3]]
                    hb_list.append(dict(tiles=tiles, w=wt[:, h, 0:256], wrep=2, wn=256))
                for c in range(4):
                    tiles = []
                    for b in range(2):
                        q_ap = qt4[hbp:hbp + 64, c, 128 * b:128 * b + 128]
                        for ti in range(2):
                            k = b + ti
                            k_ap = kt4[hbp:hbp + 64, c, 192 + 128 * k:192 + 128 * k + 128]
                            v_ap = va4[:, c, k, vcol:vcol + 128]
                            outs = [(b, ps[:, ss(accb + 512 * b + c, 128, 4)], 0, 128)]
                            tiles.append(dict(slot=(2 * b + ti) * 128, nq=128, nk=128, k=k_ap, q=q_ap,
                                              v=v_ap, outs=outs, vres=("va4", c), qkres=["kt4", "qt4"]))
                    tiles = [tiles[0], tiles[2], tiles[1], tiles[3]]
                    hb_list.append(dict(tiles=tiles, w=wt[:, h, 256:512], wrep=2, wn=256))
                for cg in range(4):
                    tiles = []
                    for ci in range(4):
                        c = 4 * cg + ci
                        q_ap = qt16[hbp:hbp + 64, c, :]
                        for ti in range(2):
                            if ti == 0:
                                nk = 128
                                k_ap = kt16[hbp:hbp + 64, c, 0:128]
                                v_ap = va16a[:, c, vcol:vcol + 128]
                                vres = "va16a"
                            else:
                                nk = 64
                                k_ap = kt16[hbp:hbp + 64, c, 128:192]
                                v_ap = va16b[0:64, c, vcol:vcol + 128]
                                vres = "va16b"
                            outs = [(q2, ps[:, ss(accb + 512 * q2 + c, 32, 16)], 32 * q2, 32)
                                    for q2 in range(2)]
                            tiles.append(dict(slot=(2 * ci + ti) * 64, nq=64, nk=nk, k=k_ap, q=q_ap,
                                              v=v_ap, outs=outs, vres=vres, qkres=["kt16", "qt16"]))
                    tiles = [t_ for t_ in tiles if t_["nk"] == 128] + [t_ for t_ in tiles if t_["nk"] == 64]
                    hb_list.append(dict(tiles=tiles, w=wt[:, h, 512:640], wrep=4, wn=128))
                for bi_, bt in enumerate(hb_list):
                    bt.update(hp=hp, hl=hl, aset=aset, sl=sl, first=first, accb=accb, hbp=hbp,
                              last=(bi_ == len(hb_list) - 1))
                    batches.append(bt)

        LAG = 3
        nb = len(batches)

        def emit_qk(i):
            bt = batches[i]
            sbk = 4 + i % 4
            sl = bt["sl"]
            for t in bt["tiles"]:
                S.pe(lambda e, t=t, sbk=sbk: e.matmul(
                    ps[0:t["nk"], 512 * sbk + t["slot"]:512 * sbk + t["slot"] + t["nq"]],
                    lhsT=t["k"], rhs=t["q"], start=True, stop=True, skip_group_check=True),
                    reads=t["qkres"], writes=[("ps", sbk)])
            ei = i % NSL
            S.act(lambda e, sbk=sbk, ei=ei: e.activation(out=et[ei], in_=bank(sbk), func=AF.Exp),
                  reads=[("ps", sbk)], writes=[("et", ei)])
            wn, wrep = bt["wn"], bt["wrep"]
            S.dve(lambda e, ei=ei, bt=bt, wn=wn, wrep=wrep: e.tensor_tensor(
                out=pt[ei].rearrange("p (a b) -> p a b", b=wn),
                in0=et[ei].rearrange("p (a b) -> p a b", b=wn),
                in1=bcast_mid(bt["w"], wrep), op=ALU.mult),
                reads=[("et", ei), "wt"], writes=[("pt", ei)])

        def emit_pv(i):
            bt = batches[i]
            ei = i % NSL
            aset = bt["aset"]
            for t in bt["tiles"]:
                for (half, o_ap, poff, n) in t["outs"]:
                    bk = 2 * aset + half
                    st = bt["first"][half]
                    bt["first"][half] = False
                    S.pe(lambda e, t=t, o_ap=o_ap, poff=poff, n=n, st=st, ei=ei: e.matmul(
                        o_ap, lhsT=t["v"],
                        rhs=pt[ei][0:t["nk"], t["slot"] + poff:t["slot"] + poff + n],
                        start=st, stop=True, skip_group_check=True),
                        reads=[("pt", ei), t["vres"]], writes=[("ps", bk)])
            if bt["last"]:
                hp, hbp, accb = bt["hp"], bt["hbp"], bt["accb"]
                S.act(lambda e, aset=aset, accb=accb: e.activation(
                    out=rc[aset][64:128, :], in_=ps[64:128, accb:accb + 1024], func=AF.Ln),
                    reads=[("ps", 2 * aset), ("ps", 2 * aset + 1)], writes=[("rc", aset)])
                S.act(lambda e, aset=aset: e.activation(
                    out=rc[aset][64:128, :], in_=rc[aset][64:128, :], func=AF.Exp, scale=-1.0),
                    reads=[("rc", aset)], writes=[("rc", aset)])
                S.dve(lambda e, aset=aset, accb=accb, hp=hp, hbp=hbp: e.tensor_tensor(
                    out=actT[hbp:hbp + 64, hp, :], in0=ps[0:64, accb:accb + 1024],
                    in1=rc[aset][64:128, :], op=ALU.mult),
                    reads=[("ps", 2 * aset), ("ps", 2 * aset + 1), ("rc", aset)],
                    writes=[("actT", hp, hbp)])

        def step_qk(i):
            if i < nb:
                for fn in pre_qk.get(i, ()):
                    fn()
                emit_qk(i)
                for fn in post_qk.get(i, ()):
                    fn()

        def step_pv(i):
            if 0 <= i < nb:
                emit_pv(i)
                for fn in post_pv.get(i, ()):
                    fn()

        for i in range(0, nb + 4, 4):
            for k_ in range(4):
                step_qk(i + k_)
            for k_ in range(4):
                step_pv(i - 4 + k_)

        for g in range(4):
            for dtl in range(2):
                for half in range(2):
                    pb = next_bank(0, 8)
                    for cc in range(2):
                        S.pe(lambda e, pb=pb, g=g, cc=cc, dtl=dtl, half=half: e.matmul(
                            bank(pb), lhsT=pw[:, g, cc, dtl * 128:dtl * 128 + 128],
                            rhs=pooledT[:, 2 * g + cc, half * 512:half * 512 + 512],
                            start=(cc == 0), stop=(cc == 1)),
                            reads=["pw", ("pooledT", 2 * g), ("pooledT", 2 * g + 1)], writes=[("ps", pb)])
                    mo = 2 * g + dtl
                    S.act(lambda e, pb=pb, mo=mo, half=half: e.activation(
                        out=actT[:, 8 + mo, half * 512:half * 512 + 512], in_=bank(pb), func=AF.Copy,
                        scale=pscale[:, mo:mo + 1]),
                        reads=[("ps", pb), "pscale"], writes=[("actT", 8 + mo, half)])

        S.barrier()
        actT_all = [("actT", hp, hbp) for hp in range(8) for hbp in (0, 64)] + \
                   [("actT", 8 + mo, half) for mo in range(8) for half in range(2)]
        wo = [wbA[0], wbA[1], wbB_o[0], wbB_o[1]]
        wo_res = [("wbA", 0), ("wbA", 1), ("wbB", 0), ("wbB", 1)]

        def load_wo(n):
            S.dma("sp", lambda e: e.dma_start(out=wo[n], in_=WOB_r[:, :, n * 512:n * 512 + 512]),
                  reads=[("WOB", q_) for q_ in range(4)], writes=[wo_res[n]], semkey=("wo", n))

        def load_x1(j):
            S.dma("sp", lambda e: e.dma_start(
                out=x1[:, j, :], in_=xr_r[:, (R0 + 1024) // 128 + j, :]),
                writes=[("x1", j)], semkey=("x1", j))

        load_wo(0)
        load_x1(0)
        load_x1(1)
        load_wo(1)
        for j in range(2, 8):
            load_x1(j)
        load_wo(2)
        load_wo(3)

        def oproj(j, n):
            pb = next_bank(0, 8)
            for kc in range(16):
                S.pe(lambda e, pb=pb, kc=kc: e.matmul(
                    bank(pb), lhsT=actT[:, kc, j * 128:j * 128 + 128], rhs=wo[n][:, kc, :],
                    start=(kc == 0), stop=(kc == 15)),
                    reads=actT_all + [wo_res[n]], writes=[("ps", pb)])
            S.dve(lambda e, pb=pb: e.tensor_tensor(
                out=x1[:, j, n * 512:n * 512 + 512], in0=bank(pb), in1=x1[:, j, n * 512:n * 512 + 512],
                op=ALU.add), reads=[("ps", pb), ("x1", j)], writes=[("x1", j)])

        def ln2_tile(j):
            hs = j % 4
            S.dve(lambda e: e.memset(ssq[:, 0, j:j + 1], 0.0), writes=[("ssq", 0, j)])
            S.act(lambda e: e.activation(out=junkB, in_=x1[:, j, :], func=AF.Square,
                                         accum_out=ssq[:, 0, j:j + 1]),
                  reads=[("x1", j)], writes=[("ssq", 0, j), "junk"])
            S.dve(lambda e: e.tensor_scalar(out=rstd[:, 0, j:j + 1], in0=ssq[:, 0, j:j + 1],
                                            scalar1=1.0 / D, scalar2=EPS, op0=ALU.mult, op1=ALU.add),
                  reads=[("ssq", 0, j)], writes=[("rstd2", j)])
            S.act(lambda e: e.activation(out=rstd[:, 0, j:j + 1], in_=rstd[:, 0, j:j + 1], func=AF.Ln),
                  reads=[("rstd2", j)], writes=[("rstd2", j)])
            S.act(lambda e: e.activation(out=rstd[:, 0, j:j + 1], in_=rstd[:, 0, j:j + 1], func=AF.Exp,
                                         scale=-0.5), reads=[("rstd2", j)], writes=[("rstd2", j)])
            S.dve(lambda e: e.scalar_tensor_tensor(
                out=hb2[hs], in0=x1[:, j, :], scalar=rstd[:, 0, j:j + 1], in1=g2,
                op0=ALU.mult, op1=ALU.mult),
                reads=[("x1", j), ("rstd2", j), "g2"], writes=[("hb2", hs)])

        def ln2_T(j):
            hs = j % 4
            for kh in range(2):
                tb = next_bank(0, 8)
                for k8 in range(8):
                    kc = 8 * kh + k8
                    S.pe(lambda e, tb=tb, k8=k8, kc=kc: e.transpose(
                        out=bankbf(tb)[:, k8 * 128:k8 * 128 + 128],
                        in_=hb2[hs][:, kc * 128:kc * 128 + 128], identity=ident[:]),
                        reads=[("hb2", hs), "ident"], writes=[("ps", tb)])
                evac_copy(actT[:, 8 * kh:8 * kh + 8, j * 128:j * 128 + 128],
                          bankbf(tb).rearrange("p (a b) -> p a b", b=128),
                          reads=[("ps", tb)], writes=[("h2T", j, kh)])

        for n in range(2):
            for j in range(8):
                oproj(j, n)
        for j in range(8):
            oproj(j, 2)
            oproj(j, 3)
            ln2_tile(j)
            if j >= 3:
                ln2_T(j - 3)
        for j in range(5, 8):
            ln2_T(j)
        h2T_all = [("h2T", j, kh) for j in range(8) for kh in range(2)]

        NG = DFF // 512

        def ffn_load(gi):
            ws = gi % 2
            S.dma("pool", lambda e, gi=gi, ws=ws: e.dma_start(
                out=wbA[ws], in_=w1_r[:, :, gi * 512:gi * 512 + 512]), writes=[("wbA", ws)],
                semkey=("dma", "pool", ("wbA", ws)))
            S.dma("pool", lambda e, gi=gi, ws=ws: e.dma_start(
                out=wbB[ws], in_=w2_r[:, gi, :, :]), writes=[("wbB", ws)])

        rl_ctr = [0]

        def ffn_A(gi):
            ws = gi % 2
            for fc in range(4):
                for half in range(2):
                    pb = next_bank(0, 8)
                    for kc in range(16):
                        S.pe(lambda e, pb=pb, kc=kc, fc=fc, half=half, ws=ws: e.matmul(
                            bank(pb), lhsT=wbA[ws][:, kc, fc * 128:fc * 128 + 128],
                            rhs=actT[:, kc, half * 512:half * 512 + 512],
                            start=(kc == 0), stop=(kc == 15)),
                            reads=h2T_all + [("wbA", ws)], writes=[("ps", pb)])
                    ri = rl_ctr[0] % 2
                    rl_ctr[0] += 1
                    S.act(lambda e, pb=pb, ri=ri: e.activation(out=rl[ri], in_=bank(pb), func=AF.Relu),
                          reads=[("ps", pb)], writes=[("rl", ri)])
                    ew_tt(dict(out=aT[ws][:, fc, half * 512:half * 512 + 512], in0=rl[ri], in1=rl[ri],
                               op=ALU.mult), [("rl", ri)], [("aT", ws, fc, half)])

        def final_tile(j):
            S.dve(lambda e: e.memset(ssq[:, 1, j:j + 1], 0.0), writes=[("ssq", 1, j)])
            S.act(lambda e: e.activation(out=junkB, in_=x1[:, j, :], func=AF.Square,
                                         accum_out=ssq[:, 1, j:j + 1]),
                  reads=[("x1", j)], writes=[("ssq", 1, j), "junk"])
            S.dve(lambda e: e.tensor_scalar(out=rstd[:, 1, j:j + 1], in0=ssq[:, 1, j:j + 1],
                                            scalar1=1.0 / D, scalar2=EPS, op0=ALU.mult, op1=ALU.add),
                  reads=[("ssq", 1, j)], writes=[("rstdf", j)])
            S.act(lambda e: e.activation(out=rstd[:, 1, j:j + 1], in_=rstd[:, 1, j:j + 1], func=AF.Ln),
                  reads=[("rstdf", j)], writes=[("rstdf", j)])
            S.act(lambda e: e.activation(out=rstd[:, 1, j:j + 1], in_=rstd[:, 1, j:j + 1], func=AF.Exp,
                                         scale=-0.5), reads=[("rstdf", j)], writes=[("rstdf", j)])
            S.dve(lambda e: e.scalar_tensor_tensor(
                out=x1[:, j, :], in0=x1[:, j, :], scalar=rstd[:, 1, j:j + 1], in1=gf,
                op0=ALU.mult, op1=ALU.mult),
                reads=[("x1", j), ("rstdf", j), "gf"], writes=[("x1", j)])
            S.dma("sp", lambda e: e.dma_start(out=y_r[:, O0 // 128 + j, :], in_=x1[:, j, :]),
                  reads=[("x1", j)], writes=[("y", u, j)], semkey=("y", j))

        def ffn_Y(gi):
            ws = gi % 2
            for j in range(8):
                if gi == NG - 1 and j >= 2:
                    final_tile(j - 2)
                for n in range(4):
                    pb = next_bank(0, 8)
                    for fc in range(4):
                        S.pe(lambda e, pb=pb, fc=fc, j=j, n=n, ws=ws: e.matmul(
                            bank(pb), lhsT=aT[ws][:, fc, j * 128:j * 128 + 128],
                            rhs=wbB[ws][:, fc, n * 512:n * 512 + 512],
                            start=(fc == 0), stop=(fc == 3)),
                            reads=[("aT", ws, fc, j // 4), ("wbB", ws)], writes=[("ps", pb)])
                    S.dve(lambda e, pb=pb, j=j, n=n: e.tensor_tensor(
                        out=x1[:, j, n * 512:n * 512 + 512], in0=bank(pb),
                        in1=x1[:, j, n * 512:n * 512 + 512], op=ALU.add),
                        reads=[("ps", pb), ("x1", j)], writes=[("x1", j)])

        S.dve(lambda e: e.memset(ssq[:, 1, 0:1], 0.0), reads=[("hb2", i_) for i_ in range(4)] + h2T_all,
              writes=[("aT", ws_, fc_, hf_) for ws_ in range(2) for fc_ in range(4) for hf_ in range(2)])
        ffn_load(0)
        ffn_A(0)
        for gi in range(NG):
            if gi + 1 < NG:
                ffn_load(gi + 1)
                ffn_A(gi + 1)
            ffn_Y(gi)

        final_tile(6)
        final_tile(7)

    for u_, R0_ in enumerate((0, 1024, 4096)):
        do_unit(u_, R0_)

    S.emit()
    stack.close()
    return nc


def _alibi_slopes():
    return np.array([2.0 ** (-8.0 * (h + 1) / 16) for h in range(16)], dtype=np.float64)


def _wtab():
    sl = _alibi_slopes()
    i = np.arange(128)[:, None].astype(np.float64)
    w = np.zeros((128, 16, WT_COLS), dtype=np.float32)
    for h in range(16):
        col = 0
        for d in (1, 4):
            j = np.arange(128)[None, :].astype(np.float64)
            a = np.exp(-sl[h] * d * np.abs(j - i + 64)) * (j <= i)
            b = np.exp(-sl[h] * d * np.abs(j - i - 64)) * (j >= i)
            w[:, h, col:col + 128] = a
            w[:, h, col + 128:col + 256] = b
            col += 256
        j = np.arange(64)[None, :].astype(np.float64)
        a = np.exp(-sl[h] * 16 * np.abs(j - i + 64)) * (j <= i)
        b = np.exp(-sl[h] * 16 * np.abs(j - i - 64)) * (j >= i) * (i < 64)
        w[:, h, col:col + 64] = a
        w[:, h, col + 64:col + 128] = b
    return w.reshape(128, 16 * WT_COLS)


def _rcnt_row(pos, T):
    out = np.zeros((4, len(pos)), dtype=np.float32)
    for g, w in enumerate((2, 4, 8, 16)):
        lo = np.clip(pos - w // 2, 0, T)
        hi = np.clip(pos + w // 2, 0, T)
        out[g] = 1.0 / (hi - lo).astype(np.float32)
    return out


_NC_CACHE = {}


def kernel(x_prompt, x_sample, ln1_g, w_in, pool_w, pool_scale, w_out, ln2_g, w1, w2, final_g):
    x_prompt = np.asarray(x_prompt, dtype=np.float32)
    x_sample = np.asarray(x_sample, dtype=np.float32)
    f = lambda a: np.ascontiguousarray(np.asarray(a, dtype=np.float32))
    SEQ = x_prompt.shape[1]
    DSEQ = x_sample.shape[1]
    if "nc" not in _NC_CACHE:
        _NC_CACHE["nc"] = build_program()
    nc = _NC_CACHE["nc"]
    rep = lambda v: np.ascontiguousarray(np.broadcast_to(f(v).reshape(1, D), (128, D)))
    common = {
        "wtab": _wtab(),
        "g1": rep(ln1_g), "g2": rep(ln2_g), "gf": rep(final_g),
        "w_in": f(w_in).reshape(D, 4096), "w_out": f(w_out).reshape(D, D),
        "w1": f(w1).reshape(D, DFF), "w2": f(w2).reshape(DFF, D),
        "pool_w": f(pool_w).reshape(4, 256, 256),
        "pscale": np.ascontiguousarray(f(pool_scale).reshape(8, 128).T),
        "ident": np.eye(128, dtype=np.float32),
    }
    in_maps = []
    for c in range(NCORES):
        xr = np.zeros((NREG, D), dtype=np.float32)
        valid = np.zeros((NREG,), dtype=np.float32)
        p0 = 2048 * c - 1024
        lo, hi = max(p0, 0), min(p0 + 4096, SEQ)
        xr[lo - p0:hi - p0] = x_prompt[0, lo:hi]
        valid[lo - p0:hi - p0] = 1.0
        s, hf = c // 2, c % 2
        seq = x_sample[s][::-1] if hf else x_sample[s]
        xr[4096 + 1024:4096 + 3072] = seq
        valid[4096 + 1024:4096 + 3072] = 1.0
        vmask = np.ascontiguousarray(valid.reshape(56, 128).T)
        pos_p = np.arange(2048 * c, 2048 * c + 2048)
        pos_s = (DSEQ - 1 - np.arange(1024)) if hf else np.arange(1024)
        rc = np.concatenate([_rcnt_row(pos_p, SEQ), _rcnt_row(pos_s, DSEQ)], axis=1)
        rcnt = np.ascontiguousarray(np.broadcast_to(rc.reshape(1, 4 * NOWN), (128, 4 * NOWN)))
        m = dict(common)
        rcB = np.zeros((4, 1024), dtype=np.float32)
        if hf:
            rcB[:] = rc[:, 2048:3072]
            rc[:, 2048:3072] = 0.0
        rcnt = np.ascontiguousarray(np.broadcast_to(rc.reshape(1, 4 * NOWN), (128, 4 * NOWN)))
        rcntB = np.ascontiguousarray(np.broadcast_to(rcB.reshape(1, 4 * 1024), (128, 4 * 1024)))
        m.update({"xr": xr, "vmask": vmask, "rcnt": rcnt, "rcntB": rcntB})
        in_maps.append(m)
    res = run_bass_kernel_spmd(nc, in_maps, core_ids=list(range(NCORES)))
    y_prompt = np.zeros_like(x_prompt)
    y_sample = np.zeros_like(x_sample)
    for c in range(NCORES):
        yc = np.asarray(res.results[c]["y"], dtype=np.float32)
        y_prompt[0, 2048 * c:2048 * c + 2048] = yc[0:2048]
        s, hf = c // 2, c % 2
        if hf:
            y_sample[s, 1024:2048] = yc[2048:3072][::-1]
        else:
            y_sample[s, 0:1024] = yc[2048:3072]
    return (y_prompt, y_sample)
```
